# Optimizing a Trainium2 kernel written in Bass

```python
import math
import jax
import jax.numpy as jnp
from jax import lax
import numpy as np


D_MODEL = 2048
BATCH = 4
SEQ = 2048
DEPTH = 4

N_MIXERS = 3
N_META = 16
GRID_W = 64
NA_WIN_ROWS = 8
NA_WIN_COLS = 16
Q_BLOCK = 128

A_HEADS = 16
A_HEAD_DIM = D_MODEL // A_HEADS
A_WIDTH = A_HEADS * A_HEAD_DIM
B_HEADS = 8
B_HEAD_DIM = D_MODEL // (2 * B_HEADS)
B_WIDTH = 2 * B_HEADS * B_HEAD_DIM
C_HEADS = 16
C_KV_HEADS = 4
C_HEAD_DIM = D_MODEL // C_HEADS
C_WIDTH = C_HEADS * C_HEAD_DIM
C_WINDOW = 128

N_A = (DEPTH + 2) // 3
N_B = (DEPTH + 1) // 3
N_C = DEPTH // 3

DEEPNORM_ALPHA = (2 * DEPTH) ** 0.25
DEEPNORM_BETA = (8 * DEPTH) ** -0.25
LN_EPS = 1e-5
RMS_EPS = 1e-5
NEG_INF = -1e30

kernel_name = 'hybrid_na_diff_swa_deepnorm_encoder'

F32 = jnp.float32


def _layer_norm(x, g, b):
    xf = x.astype(F32)
    mu = jnp.mean(xf, -1, keepdims=True)
    var = jnp.mean(jnp.square(xf - mu), -1, keepdims=True)
    y = (xf - mu) * lax.rsqrt(var + LN_EPS) * g.astype(F32) + b.astype(F32)
    return y.astype(x.dtype)


def _alibi_slopes(n_heads):
    return jnp.exp2(-8.0 * jnp.arange(1, n_heads + 1, dtype=F32) / n_heads)


def _na_branch(h, w_in, rpb):
    Bn, L, _ = h.shape
    S = L - N_META
    rows = S // GRID_W
    wr = min(NA_WIN_ROWS, rows)
    H, dh = A_HEADS, A_HEAD_DIM
    proj = jnp.einsum('bld,de->ble', h, w_in)
    q, k, v, z = jnp.split(proj, 4, axis=-1)
    q = q.reshape(Bn, L, H, dh) * (dh ** -0.5)
    k = k.reshape(Bn, L, H, dh)
    v = v.reshape(Bn, L, H, dh)
    qm, km, vm = q[:, :N_META], k[:, :N_META], v[:, :N_META]
    s_mm = jnp.einsum('bqhd,bkhd->bhqk', qm, km).astype(F32)
    p_mm = jax.nn.softmax(s_mm, -1).astype(v.dtype)
    o_meta = jnp.einsum('bhqk,bkhd->bqhd', p_mm, vm)
    qg = q[:, N_META:].reshape(Bn, rows, GRID_W, H, dh)
    kg = k[:, N_META:].reshape(Bn, rows, GRID_W, H, dh)
    vg = v[:, N_META:].reshape(Bn, rows, GRID_W, H, dh)
    row_start = jnp.clip(jnp.arange(rows) - wr // 2, 0, rows - wr)
    col_pos = jnp.arange(GRID_W)
    col_start = jnp.clip(col_pos - NA_WIN_COLS // 2, 0, GRID_W - NA_WIN_COLS)
    col_idx = col_start[:, None] + jnp.arange(NA_WIN_COLS)[None]
    rpb_c = rpb[:, :, col_idx - col_pos[:, None] + NA_WIN_COLS - 1]

    def row_block(r):
        r0 = row_start[r]
        k_rows = lax.dynamic_slice_in_dim(kg, r0, wr, axis=1)
        v_rows = lax.dynamic_slice_in_dim(vg, r0, wr, axis=1)
        k_win = k_rows[:, :, col_idx]
        v_win = v_rows[:, :, col_idx]
        q_row = lax.dynamic_index_in_dim(qg, r, axis=1, keepdims=False)
        dr = r0 + jnp.arange(wr) - r
        bias = jnp.transpose(rpb_c[:, dr + NA_WIN_ROWS - 1], (0, 2, 1, 3))
        s_win = jnp.einsum('bqhd,biqjhd->bhqij', q_row, k_win).astype(F32) + bias[None].astype(F32)
        s_meta = jnp.einsum('bqhd,bkhd->bhqk', q_row, km).astype(F32)
        logits = jnp.concatenate([s_meta, s_win.reshape(Bn, H, GRID_W, wr * NA_WIN_COLS)], -1)
        p = jax.nn.softmax(logits, -1).astype(v.dtype)
        p_meta = p[..., :N_META]
        p_win = p[..., N_META:].reshape(Bn, H, GRID_W, wr, NA_WIN_COLS)
        return (jnp.einsum('bhqk,bkhd->bqhd', p_meta, vm)
                + jnp.einsum('bhqij,biqjhd->bqhd', p_win, v_win))

    o_grid = lax.map(row_block, jnp.arange(rows))
    o_grid = jnp.moveaxis(o_grid, 0, 1).reshape(Bn, S, H, dh)
    o = jnp.concatenate([o_meta, o_grid], 1).reshape(Bn, L, A_WIDTH)
    return o * jax.nn.silu(z)


def _diff_branch(h, w_in, lq1, lk1, lq2, lk2, subln_g, layer_idx):
    Bn, L, _ = h.shape
    S = L - N_META
    H, dh = B_HEADS, B_HEAD_DIM
    lambda_init = 0.8 - 0.6 * math.exp(-0.3 * layer_idx)
    proj = jnp.einsum('bld,de->ble', h, w_in)
    q, k, v, z = jnp.split(proj, 4, axis=-1)
    q = q.reshape(Bn, L, H, 2, dh) * (dh ** -0.5)
    k = k.reshape(Bn, L, H, 2, dh)
    v = v.reshape(Bn, L, H, 2 * dh)
    lam = (jnp.exp(jnp.sum(lq1.astype(F32) * lk1.astype(F32)))
           - jnp.exp(jnp.sum(lq2.astype(F32) * lk2.astype(F32))) + lambda_init)
    slopes = _alibi_slopes(H)
    key_pos = jnp.arange(S)

    def attend(q_blk, bias):
        s = jnp.einsum('bqhjd,bkhjd->bhjqk', q_blk, k).astype(F32) + bias[None, :, None]
        p = jax.nn.softmax(s, -1)
        a = p[:, :, 0] - lam * p[:, :, 1]
        return jnp.einsum('bhqk,bkhe->bqhe', a.astype(v.dtype), v)

    o_meta = attend(q[:, :N_META], jnp.zeros((H, N_META, L), F32))
    n_blk = S // Q_BLOCK
    q_real = q[:, N_META:].reshape(Bn, n_blk, Q_BLOCK, H, 2, dh)

    def block(n):
        q_blk = lax.dynamic_index_in_dim(q_real, n, axis=1, keepdims=False)
        q_pos = n * Q_BLOCK + jnp.arange(Q_BLOCK)
        dist = jnp.abs(q_pos[:, None] - key_pos[None]).astype(F32)
        bias = jnp.concatenate([jnp.zeros((H, Q_BLOCK, N_META), F32),
                                -slopes[:, None, None] * dist[None]], -1)
        return attend(q_blk, bias)

    o_real = lax.map(block, jnp.arange(n_blk))
    o_real = jnp.moveaxis(o_real, 0, 1).reshape(Bn, S, H, 2 * dh)
    o = jnp.concatenate([o_meta, o_real], 1).astype(F32)
    o = o * lax.rsqrt(jnp.mean(jnp.square(o), -1, keepdims=True) + RMS_EPS) * subln_g.astype(F32) * (1.0 - lambda_init)
    o = o.astype(h.dtype).reshape(Bn, L, B_WIDTH)
    return o * jax.nn.silu(z)


def _swa_branch(h, w_in, sink):
    Bn, L, _ = h.shape
    S = L - N_META
    Hk, dh = C_KV_HEADS, C_HEAD_DIM
    G = C_HEADS // C_KV_HEADS
    proj = jnp.einsum('bld,de->ble', h, w_in)
    q, k, v, z = jnp.split(proj, [C_WIDTH, C_WIDTH + Hk * dh, C_WIDTH + 2 * Hk * dh], axis=-1)
    q = q.reshape(Bn, L, Hk, G, dh) * (dh ** -0.5)
    k = k.reshape(Bn, L, Hk, dh)
    v = v.reshape(Bn, L, Hk, dh)
    km, vm = k[:, :N_META], v[:, :N_META]
    sink_l = sink.astype(F32).reshape(Hk, G)
    slopes = _alibi_slopes(C_HEADS).reshape(Hk, G)

    def attend(q_blk, k_blk, v_blk, bias):
        s = jnp.einsum('bqkgd,bskd->bkgqs', q_blk, k_blk).astype(F32) + bias[None]
        sk = jnp.broadcast_to(sink_l[None, :, :, None, None], s.shape[:-1] + (1,))
        p = jax.nn.softmax(jnp.concatenate([s, sk], -1), -1)[..., :-1]
        return jnp.einsum('bkgqs,bskd->bqkgd', p.astype(v_blk.dtype), v_blk)

    o_meta = attend(q[:, :N_META], km, vm, jnp.zeros((Hk, G, N_META, N_META), F32))
    n_blk = S // Q_BLOCK
    span = Q_BLOCK + 2 * C_WINDOW
    pad = ((0, 0), (C_WINDOW, C_WINDOW), (0, 0), (0, 0))
    k_pad = jnp.pad(k[:, N_META:], pad)
    v_pad = jnp.pad(v[:, N_META:], pad)
    q_real = q[:, N_META:].reshape(Bn, n_blk, Q_BLOCK, Hk, G, dh)

    def block(n):
        q0 = n * Q_BLOCK
        q_blk = lax.dynamic_index_in_dim(q_real, n, axis=1, keepdims=False)
        k_win = lax.dynamic_slice_in_dim(k_pad, q0, span, axis=1)
        v_win = lax.dynamic_slice_in_dim(v_pad, q0, span, axis=1)
        q_pos = q0 + jnp.arange(Q_BLOCK)
        k_pos = q0 - C_WINDOW + jnp.arange(span)
        dist = jnp.abs(q_pos[:, None] - k_pos[None])
        valid = (dist <= C_WINDOW) & (k_pos[None] >= 0) & (k_pos[None] < S)
        win_bias = jnp.where(valid[None, None],
                             -slopes[:, :, None, None] * dist.astype(F32)[None, None], NEG_INF)
        bias = jnp.concatenate([jnp.zeros((Hk, G, Q_BLOCK, N_META), F32), win_bias], -1)
        k_blk = jnp.concatenate([km, k_win], 1)
        v_blk = jnp.concatenate([vm, v_win], 1)
        return attend(q_blk, k_blk, v_blk, bias)

    o_real = lax.map(block, jnp.arange(n_blk))
    o_real = jnp.moveaxis(o_real, 0, 1).reshape(Bn, S, Hk, G, dh)
    o = jnp.concatenate([o_meta, o_real], 1).reshape(Bn, L, C_WIDTH)
    return o * jax.nn.silu(z)


def setup_inputs(seed: int = 0) -> dict:
    key = jax.random.key(seed)
    ks = jax.random.split(key, 16)
    nrm = jax.random.normal
    d_in = D_MODEL ** -0.5
    x = nrm(ks[0], (BATCH, SEQ, D_MODEL), F32)
    meta_tokens = nrm(ks[1], (N_META, D_MODEL), F32)
    w_in_a = nrm(ks[2], (N_A, D_MODEL, 4 * A_WIDTH), F32) * d_in
    rpb_a = nrm(ks[3], (N_A, A_HEADS, 2 * NA_WIN_ROWS - 1, 2 * NA_WIN_COLS - 1), F32) * 0.1
    w_in_b = nrm(ks[4], (N_B, D_MODEL, 4 * B_WIDTH), F32) * d_in
    lam_q1_b = nrm(ks[5], (N_B, B_HEAD_DIM), F32) * 0.1
    lam_k1_b = nrm(ks[6], (N_B, B_HEAD_DIM), F32) * 0.1
    lam_q2_b = nrm(ks[7], (N_B, B_HEAD_DIM), F32) * 0.1
    lam_k2_b = nrm(ks[8], (N_B, B_HEAD_DIM), F32) * 0.1
    subln_g_b = 1.0 + 0.02 * nrm(ks[9], (N_B, 2 * B_HEAD_DIM), F32)
    w_in_c = nrm(ks[10], (N_C, D_MODEL, 2 * C_WIDTH + 2 * C_KV_HEADS * C_HEAD_DIM), F32) * d_in
    sink_c = nrm(ks[11], (N_C, C_HEADS), F32)
    w_out = nrm(ks[12], (DEPTH, D_MODEL, D_MODEL), F32) * (D_MODEL ** -0.5) * DEEPNORM_BETA
    ln_g = 1.0 + 0.02 * nrm(ks[13], (DEPTH, D_MODEL), F32)
    ln_b = 0.02 * nrm(ks[14], (DEPTH, D_MODEL), F32)
    return {'x': x, 'meta_tokens': meta_tokens, 'w_in_a': w_in_a, 'rpb_a': rpb_a,
            'w_in_b': w_in_b, 'lam_q1_b': lam_q1_b, 'lam_k1_b': lam_k1_b,
            'lam_q2_b': lam_q2_b, 'lam_k2_b': lam_k2_b, 'subln_g_b': subln_g_b,
            'w_in_c': w_in_c, 'sink_c': sink_c, 'w_out': w_out, 'ln_g': ln_g, 'ln_b': ln_b}


def reference(x, meta_tokens, w_in_a, rpb_a, w_in_b, lam_q1_b, lam_k1_b, lam_q2_b,
              lam_k2_b, subln_g_b, w_in_c, sink_c, w_out, ln_g, ln_b):
    Bn = x.shape[0]
    meta = jnp.broadcast_to(meta_tokens.astype(x.dtype)[None], (Bn, N_META, D_MODEL))
    h = jnp.concatenate([meta, x], axis=1)
    for i in range(DEPTH):
        kind, j = i % N_MIXERS, i // N_MIXERS
        if kind == 0:
            y = _na_branch(h, w_in_a[j], rpb_a[j])
        elif kind == 1:
            y = _diff_branch(h, w_in_b[j], lam_q1_b[j], lam_k1_b[j], lam_q2_b[j],
                             lam_k2_b[j], subln_g_b[j], i)
        else:
            y = _swa_branch(h, w_in_c[j], sink_c[j])
        out = jnp.einsum('ble,ed->bld', y, w_out[i])
        h = _layer_norm(DEEPNORM_ALPHA * h + out, ln_g[i], ln_b[i])
    return h[:, N_META:]
```

```python
import contextlib
import math
import numpy as np
import concourse.bass as bass
import concourse.mybir as mybir
from concourse.bass_utils import run_bass_kernel_spmd

F32 = mybir.dt.float32
BF16 = mybir.dt.bfloat16
AF = mybir.ActivationFunctionType
ALU = mybir.AluOpType
AX = mybir.AxisListType

D = 2048
NCH = 16
SEQ = 2048
BATCH = 4
NMETA = 16
DEPTH = 4
DH = 128
SCALE = DH ** -0.5
ALPHA = (2 * DEPTH) ** 0.25
LN_EPS = 1e-5
RMS_EPS = 1e-5
NEG = -300.0
DBIG = 1.0e5
NOWN = 8
NTOK_OWN = NOWN * 128 + NMETA

ENGS = ["pe", "act", "dve", "pool", "sp"]
NDMA = 32


class Sched:
    def __init__(self, nc):
        self.nc = nc
        self.q = {e: [] for e in ENGS}
        self.count = {e: 0 for e in ENGS}
        self.seen = {e: {} for e in ENGS}
        self.res = {}
        self.dma_next = {"sp": 0, "pool": 0, "act": 0}
        self.dma_cnt = [0] * NDMA

    def _deps(self, eng, reads, writes):
        need = {}

        def add(tok):
            if tok is None:
                return
            k, v = tok
            if need.get(k, 0) < v:
                need[k] = v

        for key in reads:
            st = self.res.get(key)
            if st:
                add(st["w"])
        for key in writes:
            st = self.res.get(key)
            if st:
                add(st["w"])
                for tok in st["r"].items():
                    add(tok)
        waits = []
        seen = self.seen[eng]
        for k, v in need.items():
            if k == eng and eng == "pe":
                continue
            if seen.get(k, 0) >= v:
                continue
            seen[k] = v
            waits.append((k, v))
        return waits

    def _mark(self, tok, reads, writes):
        k, v = tok
        for key in reads:
            st = self.res.setdefault(key, {"w": None, "r": {}})
            if st["r"].get(k, 0) < v:
                st["r"][k] = v
        for key in writes:
            self.res[key] = {"w": tok, "r": {}}

    def op(self, eng, fn, reads=(), writes=()):
        waits = self._deps(eng, reads, writes)
        self.count[eng] += 1
        idx = self.count[eng]
        self.q[eng].append(("op", fn, waits, idx))
        self._mark((eng, idx), reads, writes)
        return (eng, idx)

    def dma(self, eng, fn, reads=(), writes=()):
        half = NDMA // 2
        base = 0 if eng == "sp" else half
        slot = base + self.dma_next[eng]
        self.dma_next[eng] = (self.dma_next[eng] + 1) % half
        waits = self._deps(eng, reads, writes)
        if self.dma_cnt[slot] > 0:
            k, v = ("q%d" % slot, self.dma_cnt[slot])
            if self.seen[eng].get(k, 0) < v:
                self.seen[eng][k] = v
                waits.append((k, v))
        self.dma_cnt[slot] += 1
        tok = ("q%d" % slot, self.dma_cnt[slot])
        self.q[eng].append(("dma", fn, waits, slot))
        self._mark(tok, reads, writes)
        return tok

    def cc(self, fn, reads=(), writes=()):
        waits = self._deps("pool", reads, writes)
        self.cc_cnt = getattr(self, "cc_cnt", 0) + 1
        tok = ("cc", self.cc_cnt)
        self.q["pool"].append(("cc", fn, waits, None))
        self._mark(tok, reads, writes)
        return tok

    def barrier(self):
        for eng in ENGS:
            waits = []
            seen = self.seen[eng]
            allk = [(e, self.count[e]) for e in ENGS[:4]]
            allk += [("q%d" % i, self.dma_cnt[i]) for i in range(NDMA)]
            allk += [("cc", getattr(self, "cc_cnt", 0))]
            for k, v in allk:
                if v > 0 and seen.get(k, 0) < v:
                    seen[k] = v
                    waits.append((k, v))
            self.q[eng].append(("wait", None, waits, None))

    def wait_all(self, eng, keys):
        waits = self._deps(eng, keys, ())
        self.q[eng].append(("wait", None, waits, None))

    def emit(self):
        nc = self.nc
        with contextlib.ExitStack() as es:
            sems = {}
            for e in ENGS[:4]:
                sems[e] = es.enter_context(nc.semaphore("s_" + e))
            for i in range(NDMA):
                sems["q%d" % i] = es.enter_context(nc.semaphore("s_q%d" % i))
            sems["cc"] = es.enter_context(nc.semaphore("s_cc"))
            block = es.enter_context(nc.Block())

            def run(engname):
                def body(engine):
                    for kind, fn, waits, info in self.q[engname]:
                        for k, v in waits:
                            engine.wait_ge(sems[k], v * 16 if k.startswith("q") else v)
                        if kind == "op":
                            fn(engine).then_inc(sems[engname], 1)
                        elif kind == "dma":
                            fn(engine).then_inc(sems["q%d" % info], 16)
                        elif kind == "cc":
                            fn(engine).then_inc(sems["cc"], 1)
                return body

            block.tensor(run("pe"))
            block.scalar(run("act"))
            block.vector(run("dve"))
            block.gpsimd(run("pool"))
            block.sync(run("sp"))


def alibi_slopes(n):
    return [2.0 ** (-8.0 * (i + 1) / n) for i in range(n)]


class Cfg:
    def __init__(self, kind, layer_idx):
        self.kind = kind
        self.layer_idx = layer_idx
        if kind == 0:
            self.nunits, self.nq, self.nk, self.dv, self.zw = 16, 1, 1, 128, 128
            self.nhalo = 2
            self.dlist = [list(range(-2, 4))] + [list(range(-2, 3))] * 6 + [list(range(-3, 3))]
        elif kind == 1:
            self.nunits, self.nq, self.nk, self.dv, self.zw = 8, 2, 2, 256, 256
            self.nhalo = 8
            self.lambda_init = 0.8 - 0.6 * math.exp(-0.3 * layer_idx)
        else:
            self.nunits, self.nq, self.nk, self.dv, self.zw = 4, 4, 1, 128, 512
            self.nhalo = 2
            self.dlist = [[-1, 0, 1]] * 8
        self.nwin = NOWN + self.nhalo
        self.nwt = self.nwin * 128 + NMETA
        self.meta_off = self.nwin * 128
        self.nvb = self.dv // 128
        self.nzb = self.zw // 128
        self.blk_per_unit = self.nq + self.nk + self.nvb + self.nzb
        self.nblk = self.blk_per_unit * self.nunits

    def key_slot(self, n, dl):
        k = n + dl
        if 0 <= k < NOWN:
            return k
        if self.kind == 0:
            return {-1: 8, -2: 9, 8: 8, 9: 9}[k]
        return {-1: 8, 8: 9}[k]


def win_blocks(cfg, s):
    lo = NOWN * s
    own = list(range(lo, lo + NOWN))
    if cfg.kind == 1:
        other = list(range(NOWN * (1 - s), NOWN * (1 - s) + NOWN))
        return own + other
    if cfg.kind == 0:
        halo = [lo + NOWN, lo + NOWN + 1] if s == 0 else [lo - 1, lo - 2]
    else:
        halo = [lo - 1, lo + NOWN]
    return own + [b if 0 <= b < 16 else None for b in halo]


def unit_cols(cfg, u):
    if cfg.kind == 0:
        return [u * 128, 2048 + u * 128, 4096 + u * 128, 6144 + u * 128]
    if cfg.kind == 1:
        b = u * 256
        return [4096 + b, 4096 + b + 128, 6144 + b, 6144 + b + 128,
                b, b + 128, 2048 + b, 2048 + b + 128]
    q = [u * 512 + g * 128 for g in range(4)]
    z = [3072 + u * 512 + g * 128 for g in range(4)]
    return q + [2048 + u * 128, 2560 + u * 128] + z


def prep_w_in(cfg, w):
    cols = []
    for u in range(cfg.nunits):
        cols += unit_cols(cfg, u)
    idx = (np.asarray(cols)[:, None] + np.arange(128)[None]).reshape(-1)
    wg = np.ascontiguousarray(w[:, idx])
    wg = wg.reshape(NCH, 128, len(cols), 128).transpose(2, 1, 0, 3)
    return np.ascontiguousarray(wg).reshape(len(cols), 128, NCH * 128)


def prep_w_out(w):
    wg = w.reshape(NCH, 128, 4, 512).transpose(2, 1, 0, 3)
    return np.ascontiguousarray(wg).reshape(4, 128, NCH * 512)


def na_bias_tables(cfg, rpb, s):
    rows = 32
    kr_l = np.arange(128) // 64
    kc = np.arange(128) % 64
    qr_l = np.arange(128) // 64
    qc = np.arange(128) % 64
    cs = np.clip(qc - 8, 0, 64 - 16)
    colok = (kc[:, None] >= cs[None, :]) & (kc[:, None] < cs[None, :] + 16)
    dc = kc[:, None] - qc[None, :] + 15
    dc_c = np.clip(dc, 0, 30)
    tiles = []
    for n in range(NOWN):
        qb = NOWN * s + n
        qr = 2 * qb + qr_l
        rs = np.clip(qr - 4, 0, rows - 8)
        for dl in cfg.dlist[n]:
            kb = qb + dl
            if kb < 0 or kb > 15:
                tiles.append(np.full((16, 128, 128), NEG, np.float32))
                continue
            kr = 2 * kb + kr_l
            rowok = (kr[:, None] >= rs[None, :]) & (kr[:, None] < rs[None, :] + 8)
            dr = kr[:, None] - qr[None, :] + 7
            dr_c = np.clip(dr, 0, 14)
            vals = rpb[:, dr_c, dc_c]
            ok = (rowok & colok)[None]
            tiles.append(np.where(ok, vals, np.float32(NEG)).astype(np.float32))
    return np.ascontiguousarray(np.concatenate(tiles, axis=2))


def diff_dist_tables(s):
    C = 896
    x = np.arange(1920)[None, :]
    p = np.arange(128)[:, None]
    own = np.abs(x - C - p)
    sg = -1 if s == 0 else 1
    oth = 1024 + sg * (x - C - p)
    return np.concatenate([own, oth], axis=1).astype(np.float32)


def swa_dist_tables(s):
    p = np.arange(128)[:, None]
    q = np.arange(128)[None, :]

    def tile(dl, valid=True):
        u = q - (128 * dl + p)
        d = np.abs(u).astype(np.float32)
        d = np.where(d <= 128, d, np.float32(DBIG))
        if not valid:
            d = np.full_like(d, DBIG)
        return d

    first = [tile(-1, s == 1), tile(0), tile(1)]
    mid = [tile(-1), tile(0), tile(1)]
    last = [tile(-1), tile(0), tile(1, s == 0)]
    return np.concatenate(first + mid + last, axis=1).astype(np.float32)


KINDS = [i % 3 for i in range(DEPTH)]
PAIRS = [[0, 1], [2, 3], [4, 5], [6, 7]]


def exch_plan(kind):
    if kind == 1:
        chunks = [[0, 1, 2, 3], [4, 5, 6, 7]]
        halo = {}
        for j in range(NOWN):
            ch, k = j // 4, j % 4
            halo[NOWN + j] = [(ch, k * 128, "A"), (ch, 512 + k * 128, "B")]
        return chunks, halo
    if kind == 0:
        chunks = [[6, 7, 0, 1]]
        halo = {8: [(0, 128, "A"), (0, 512 + 256, "B")], 9: [(0, 0, "A"), (0, 512 + 384, "B")]}
        return chunks, halo
    chunks = [[7, 0]]
    halo = {8: [(0, 0, "A")], 9: [(0, 256 + 128, "B")]}
    return chunks, halo


def build_fused():
    nc = bass.Bass("TRN2", target_bir_lowering=False)
    cfgs = [Cfg(KINDS[i], i) for i in range(DEPTH)]
    dt = lambda name, shape, k="ExternalInput", d=F32: nc.dram_tensor(name, shape, d, kind=k).ap()
    hwin0 = dt("hwin0", [cfgs[0].nwt, D])
    wqs = [dt("wq%d" % i, [cfgs[i].nblk, 128, NCH * 128]) for i in range(DEPTH)]
    wos = dt("wo", [DEPTH, 4, 128, NCH * 512])
    lng = dt("lng", [DEPTH, D])
    lnb = dt("lnb", [DEPTH, D])
    idn = dt("idn", [3, 128, 128])
    hout = dt("hout", [NOWN * 128, D], "ExternalOutput")
    pre = dt("pre", [NTOK_OWN, D], "Internal")
    tbs = {}
    for i in range(DEPTH):
        if KINDS[i] == 0:
            nt_ = sum(len(x) for x in cfgs[i].dlist)
            tbs[i] = dt("tb%d" % i, [16, 128, nt_ * 128])
        elif KINDS[i] == 1:
            tbs[i] = dt("tb%d" % i, [128, 2 * 1920])
        else:
            tbs[i] = dt("tb%d" % i, [128, 9 * 128])
    lamv = dt("lamv", [4, 128])
    subg = dt("subg", [1, 256])
    sink = dt("sink", [1, 16])
    hcur = {L: dt("hcur%d" % L, [NTOK_OWN, D], "Internal") for L in range(1, DEPTH)}
    hbf = {L: dt("hbf%d" % L, [NTOK_OWN, D], "Internal", BF16) for L in range(1, DEPTH)}
    hbx, hgt = {}, {}
    for L in range(1, DEPTH):
        chunks, _ = exch_plan(KINDS[L])
        for ci, sl in enumerate(chunks):
            hbx[(L, ci)] = nc.dram_tensor("hbx%d_%d" % (L, ci), [len(sl) * 128, D], BF16)
            hgt[(L, ci)] = nc.dram_tensor("hgt%d_%d" % (L, ci), [2 * len(sl) * 128, D], BF16)

    NW = 6
    mx = lambda f: max(f(c) for c in cfgs)
    with contextlib.ExitStack() as es:
        T = lambda name, shape, d=F32: es.enter_context(nc.sbuf_tensor(name, shape, d))
        P = lambda name, shape, d=F32: es.enter_context(nc.psum_tensor(name, shape, d))
        hT_f = T("hT", [128, NCH * mx(lambda c: c.nwt)], BF16)
        yT = T("yT", [128, NCH, NTOK_OWN], BF16)
        wp = T("wp", [128, NW, NCH * 128], BF16)
        qT_f = T("qT", [128, mx(lambda c: c.nq) * NTOK_OWN], BF16)
        kT_f = T("kT", [128, mx(lambda c: c.nk * c.nwt)], BF16)
        V_f = T("V", [128, mx(lambda c: (c.nwin + 1) * (c.dv + 2))], BF16)
        sz_f = T("sz", [128, mx(lambda c: (NOWN + 1) * c.zw)], BF16)
        PT = T("PT", [128, 4, 512], BF16)
        PTm = None
        tt = T("tt", [128, 4, 512], F32)
        vz1 = T("vz1", [128, 6700], BF16)
        stg = vz1[:, 0:2 * D].rearrange("p (a b) -> p a b", a=2)
        ident3 = T("ident", [128, 3, 128], BF16)
        ybuf = T("ybuf", [128, 2, 256], BF16)
        small = T("small", [128, 64], F32)
        ez = T("ez", [128, 512], F32)
        NTB = 42 * 128
        tbl_f = T("tbl", [128, 2 * 1920], F32)
        obuf = T("obuf", [128, 2, 2, 256], F32)
        lamt = T("lamt", [128, 4, 128], F32)
        gl = T("gl", [128, 256], F32)
        junk = T("junk", [128, 256], F32)
        esink = T("esink", [128, 16], F32)
        rbuf = qT_f[:].bitcast(F32)[:, 0:1024].rearrange("p (a b) -> p a b", a=2)
        rbuf2 = kT_f[:].bitcast(F32)[:, 0:1024].rearrange("p (a b) -> p a b", a=2)

        pp = P("pp", [128, 2, 512], F32)
        sc = P("sc", [128, 4, 512], F32)
        pv = P("pv", [128, 2, 512], F32)
        ppb = pp[:].bitcast(BF16)
        ident = ident3[:, 0, :]
        maskid = {"A": ident3[:, 1, :], "B": ident3[:, 2, :]}

        S = Sched(nc)
        cnt = {"pp": 0, "sc": 0, "pv": 0, "wp": 0, "stg": 0, "y": 0, "x": 0, "r": 0}

        def nxt(name, n):
            v = cnt[name] % n
            cnt[name] += 1
            return v

        S.dma("pool", lambda e: e.dma_start(out=ident3[:], in_=idn.rearrange("a p j -> p a j")),
              writes=["ident"])

        for L in range(DEPTH):
            emit_layer(nc, S, L, cfgs[L], locals())
            if L < DEPTH - 1:
                S.barrier()
        S.wait_all("sp", [("hout", ti) for ti in range(NOWN)])
        S.emit()
    return nc


def emit_layer(nc, S, L, cfg, E):
    kind = cfg.kind
    NWT = cfg.nwt
    nxt = E["nxt"]
    hT_f, yT, wp, qT_f, kT_f, V_f, sz_f = E["hT_f"], E["yT"], E["wp"], E["qT_f"], E["kT_f"], E["V_f"], E["sz_f"]
    PT, PTm, tt, stg, ident, maskid, ybuf, small = (E["PT"], E["PTm"], E["tt"], E["stg"], E["ident"],
                                                   E["maskid"], E["ybuf"], E["small"])
    tbl_f, obuf, lamt, gl, junk, esink, rbuf, rbuf2 = (E["tbl_f"], E["obuf"], E["lamt"], E["gl"], E["junk"],
                                                       E["esink"], E["rbuf"], E["rbuf2"])
    vz1 = E["vz1"]
    ez = E["ez"]
    pp, sc, pv, ppb = E["pp"], E["sc"], E["pv"], E["ppb"]
    wq, wo, tb = E["wqs"][L], E["wos"][L], E["tbs"][L]
    lng, lnb, pre, hout = E["lng"], E["lnb"], E["pre"], E["hout"]
    lamv, subg, sink = E["lamv"], E["subg"], E["sink"]
    hwin0, hcur, hbf, hbx, hgt = E["hwin0"], E["hcur"], E["hbf"], E["hbx"], E["hgt"]
    NW = E["NW"]
    NTB = E["NTB"]
    last = (L == DEPTH - 1)

    hT = hT_f[:, 0:NCH * NWT].rearrange("p (c t) -> p c t", c=NCH)
    qT = qT_f[:, 0:cfg.nq * NTOK_OWN].rearrange("p (m t) -> p m t", m=cfg.nq)
    kT = kT_f[:, 0:cfg.nk * NWT].rearrange("p (m t) -> p m t", m=cfg.nk)
    V = V_f[:, 0:(cfg.nwin + 1) * (cfg.dv + 2)].rearrange("p (s d) -> p s d", s=cfg.nwin + 1)
    sz = sz_f[:, 0:(NOWN + 1) * cfg.zw].rearrange("p (t z) -> p t z", t=NOWN + 1)
    UBQ = [(qT, kT)]
    UBV = [(V, sz)]
    if kind != 1:
        o0 = NCH * NWT
        n_q, n_k = cfg.nq * NTOK_OWN, cfg.nk * NWT
        n_v, n_z = (cfg.nwin + 1) * (cfg.dv + 2), (NOWN + 1) * cfg.zw
        assert o0 + n_q + n_k + n_v + n_z <= NCH * 2064
        qT1 = hT_f[:, o0:o0 + n_q].rearrange("p (m t) -> p m t", m=cfg.nq)
        kT1 = hT_f[:, o0 + n_q:o0 + n_q + n_k].rearrange("p (m t) -> p m t", m=cfg.nk)
        V1 = hT_f[:, o0 + n_q + n_k:o0 + n_q + n_k + n_v].rearrange("p (s d) -> p s d", s=cfg.nwin + 1)
        sz1 = hT_f[:, o0 + n_q + n_k + n_v:o0 + n_q + n_k + n_v + n_z].rearrange(
            "p (t z) -> p t z", t=NOWN + 1)
        UBQ.append((qT1, kT1))
        UBV.append((V1, sz1))
    else:
        n_v, n_z = (cfg.nwin + 1) * (cfg.dv + 2), (NOWN + 1) * cfg.zw
        V1 = vz1[:, 0:n_v].rearrange("p (s d) -> p s d", s=cfg.nwin + 1)
        sz1 = vz1[:, n_v:n_v + n_z].rearrange("p (t z) -> p t z", t=NOWN + 1)
        UBV.append((V1, sz1))
    NPAR = len(UBQ)
    tbl = tbl_f
    if kind == 0:
        ntile_tot = sum(len(x) for x in cfg.dlist)
        tbl_bf = tbl_f[:].bitcast(BF16)
        nhalf = sum(len(x) for x in cfg.dlist[:4]) * 128

    if L > 0:
        xch_in, _ = exch_plan(kind)
        for ci, sl in enumerate(xch_in):
            S.cc(lambda e, ci=ci: e.collective_compute(
                "AllGather", ALU.bypass, replica_groups=PAIRS,
                ins=[hbx[(L, ci)].ap().opt()], outs=[hgt[(L, ci)].ap().opt()]),
                reads=[("hbx", L, ci, pos) for pos in range(len(sl))],
                writes=[("hgt", L, ci)])

    if kind == 1:
        S.dma("sp", lambda e: e.dma_start(out=tbl[:, 0:2 * 1920], in_=tb), writes=["tbl"])
        for i in range(4):
            S.dma("sp", lambda e, i=i: e.dma_start(
                out=lamt[:, i, :], in_=lamv[i:i + 1, :].partition_broadcast(128)), writes=["lamt"])
        S.dma("sp", lambda e: e.dma_start(out=gl[:], in_=subg.partition_broadcast(128)),
              writes=["gl"])
        S.op("act", lambda e: e.mul(gl[:], gl[:], 1.0 - cfg.lambda_init), reads=["gl"], writes=["gl"])
        for j in range(2):
            S.op("dve", lambda e, j=j: e.tensor_tensor(junk[:, 0:128], lamt[:, 2 * j, :],
                                                       lamt[:, 2 * j + 1, :], ALU.mult),
                 reads=["lamt"], writes=["junk"])
            S.op("dve", lambda e, j=j: e.tensor_reduce(small[:, 1 + j:2 + j], junk[:, 0:128],
                                                       AX.X, ALU.add),
                 reads=["junk"], writes=["small"])
        S.op("act", lambda e: e.activation(small[:, 1:3], small[:, 1:3], AF.Exp),
             reads=["small"], writes=["small"])
        S.op("dve", lambda e: e.scalar_tensor_tensor(small[:, 0:1], small[:, 2:3],
                                                     -cfg.lambda_init, small[:, 1:2],
                                                     ALU.add, ALU.subtract),
             reads=["small"], writes=["small"])
    elif kind == 2:
        S.dma("sp", lambda e: e.dma_start(out=tbl[:, 0:9 * 128], in_=tb), writes=["tbl"])
        S.dma("sp", lambda e: e.dma_start(out=esink[:], in_=sink.partition_broadcast(128)),
              writes=["esink"])
        S.op("act", lambda e: e.activation(esink[:], esink[:], AF.Exp),
             reads=["esink"], writes=["esink"])

    def tok_rows(slot):
        if slot < cfg.nwin:
            return slot * 128, 128
        return cfg.meta_off, NMETA

    def own_tok(ti):
        if ti < NOWN:
            return ti * 128, ti * 128, 128
        return cfg.meta_off, NOWN * 128, NMETA

    def transpose_rows(slot, sb, nr, rkeys):
        r0, _ = tok_rows(slot)
        for half in range(2):
            pb = nxt("pp", 2)
            for c8 in range(8):
                c = half * 8 + c8
                S.op("pe", lambda e, pb=pb, c8=c8, c=c: e.transpose(
                    ppb[:, pb, c8 * 128:c8 * 128 + nr], stg[0:nr, sb, c * 128:(c + 1) * 128],
                    ident[0:nr, 0:nr]),
                    reads=rkeys + ["ident"], writes=[("pp", pb)])
            src = lambda pb=pb: ppb[:, pb, :].rearrange("p (c t) -> p c t", c=8)[:, :, 0:nr]
            dst = lambda half=half: hT[:, half * 8:half * 8 + 8, r0:r0 + nr]
            if half == 0:
                S.op("dve", lambda e, src=src, dst=dst: e.tensor_copy(dst(), src()),
                     reads=[("pp", pb)], writes=[("hT", slot)])
            else:
                S.op("act", lambda e, src=src, dst=dst: e.copy(dst(), src()),
                     reads=[("pp", pb)], writes=[("hT", slot)])

    if L == 0:
        for slot in range(cfg.nwin + 1):
            r0, nr = tok_rows(slot)
            sb = nxt("stg", 2)
            S.dma("pool", lambda e, sb=sb, r0=r0, nr=nr: e.dma_start(
                out=stg[0:nr, sb, :], in_=hwin0[r0:r0 + nr, :]), writes=[("stg", sb)])
            transpose_rows(slot, sb, nr, [("stg", sb)])
    else:
        _, halo = exch_plan(kind)
        for ti in range(NOWN + 1):
            _, ooff, nr = own_tok(ti)
            slot = ti if ti < NOWN else cfg.nwin
            sb = nxt("stg", 2)
            S.dma("sp", lambda e, sb=sb, ooff=ooff, nr=nr: e.dma_start(
                out=stg[0:nr, sb, :], in_=hbf[L][ooff:ooff + nr, :]),
                reads=[("hbf", L, ti)], writes=[("stg", sb)])
            transpose_rows(slot, sb, nr, [("stg", sb)])
        for slot in sorted(halo):
            cands = halo[slot]
            sbs = []
            for (ci, roff, mk) in cands:
                sb = nxt("stg", 2)
                sbs.append(sb)
                S.dma("sp", lambda e, sb=sb, ci=ci, roff=roff: e.dma_start(
                    out=stg[:, sb, :], in_=hgt[(L, ci)][roff:roff + 128, :]),
                    reads=[("hgt", L, ci)], writes=[("stg", sb)])
            r0 = slot * 128
            for q4 in range(4):
                pb = nxt("pp", 2)
                for c4 in range(4):
                    c = q4 * 4 + c4
                    for k, (ci, roff, mk) in enumerate(cands):
                        S.op("pe", lambda e, pb=pb, c4=c4, c=c, k=k, mk=mk, sb=sbs[k]: e.matmul(
                            pp[:, pb, c4 * 128:(c4 + 1) * 128], stg[:, sb, c * 128:(c + 1) * 128],
                            maskid[mk], start=(k == 0), stop=(k == len(cands) - 1)),
                            reads=[("stg", sbs[k]), "ident"], writes=[("pp", pb)])
                src = lambda pb=pb: pp[:, pb, :].rearrange("p (c t) -> p c t", c=4)
                dst = lambda q4=q4, r0=r0: hT[:, q4 * 4:q4 * 4 + 4, r0:r0 + 128]
                if q4 % 2 == 0:
                    S.op("dve", lambda e, src=src, dst=dst: e.tensor_copy(dst(), src()),
                         reads=[("pp", pb)], writes=[("hT", slot)])
                else:
                    S.op("act", lambda e, src=src, dst=dst: e.copy(dst(), src()),
                         reads=[("pp", pb)], writes=[("hT", slot)])

    hT_all = [("hT", s_) for s_ in range(cfg.nwin + 1)]

    for par_ in range(len(UBV)):
        S.op("pool", lambda e, par_=par_: e.memset(UBV[par_][0][:, :, cfg.dv:cfg.dv + 2], 1.0),
             writes=[("Vones", par_), ("V", par_), ("stg", 0), ("stg", 1)])

    wb_of = {}
    w_next = [0]

    def load_w(blk):
        upto = min(blk + 3, cfg.nblk)
        while w_next[0] < upto:
            bi = w_next[0]
            wb = nxt("wp", NW)
            wb_of[bi] = wb
            S.dma("pool", lambda e, wb=wb, bi=bi: e.dma_start(out=wp[:, wb, :], in_=wq[bi]),
                  writes=[("wp", wb)])
            w_next[0] += 1
        return wb_of[blk]

    def wview(wb):
        return wp[:, wb, :].rearrange("p (c j) -> p c j", c=NCH)

    def evac(eng, dst, src, reads, writes):
        if eng == "dve":
            S.op("dve", lambda e: e.tensor_copy(dst(), src()), reads=reads, writes=writes)
        else:
            S.op("act", lambda e: e.copy(dst(), src()), reads=reads, writes=writes)

    ev_rr = [0]

    def ev_eng():
        ev_rr[0] += 1
        return "dve" if ev_rr[0] % 2 else "act"

    def proj_fm(wb, dst_fn, tok_chunks, hkeys, wkey_dst):
        w = wview(wb)
        for (off, doff, n) in tok_chunks:
            pb = nxt("pp", 2)
            for c in range(NCH):
                S.op("pe", lambda e, pb=pb, c=c, off=off, n=n, w=w: e.matmul(
                    pp[:, pb, 0:n], w[:, c, :], hT[:, c, off:off + n],
                    start=(c == 0), stop=(c == NCH - 1)),
                    reads=[("wp", wb)] + hkeys, writes=[("pp", pb)])
            evac(ev_eng(), (lambda doff=doff, n=n: dst_fn(doff, n)),
                 (lambda pb=pb, n=n: pp[:, pb, 0:n]), [("pp", pb)], [wkey_dst])
            yield

    own_chunks = [(0, 0, 512), (512, 512, 512), (cfg.meta_off, NOWN * 128, NMETA)]
    win_chunks = []
    o_ = 0
    while o_ < cfg.nwin * 128:
        n_ = min(512, cfg.nwin * 128 - o_)
        win_chunks.append((o_, o_, n_))
        o_ += n_
    win_chunks.append((cfg.meta_off, cfg.meta_off, NMETA))

    def load_na_tbl(u_, h):
        c0, c1 = (0, nhalf) if h == 0 else (nhalf, ntile_tot * 128)
        S.dma("pool", lambda e: e.dma_start(out=tbl_bf[:, c0:c1], in_=tb[u_][:, c0:c1]),
              writes=[("tbl", h)])

    n_fill = cfg.nwin + 1 + (2 + cfg.nq * len(own_chunks) + cfg.nk * len(win_chunks) if kind != 1 else 0)

    def proj_qk(u, par):
        qT, kT = UBQ[par]
        b = u * cfg.blk_per_unit + (cfg.nvb + cfg.nzb if kind == 1 else 0)
        for m in range(cfg.nq):
            wb = load_w(b); b += 1
            for _ in proj_fm(wb, (lambda doff, n, m=m: qT[:, m, doff:doff + n]), own_chunks, hT_all,
                             ("qT", par)):
                yield
        for m in range(cfg.nk):
            wb = load_w(b); b += 1
            for _ in proj_fm(wb, (lambda doff, n, m=m: kT[:, m, doff:doff + n]), win_chunks, hT_all,
                             ("kT", par)):
                yield
        yield

    def proj_vz(u, parv):
        V, sz = UBV[parv]
        b = u * cfg.blk_per_unit + (0 if kind == 1 else cfg.nq + cfg.nk)
        wvs = []
        for j in range(cfg.nvb):
            wvs.append(load_w(b)); b += 1
        for slot in range(cfg.nwin + 1):
            r0, nr = tok_rows(slot)
            pb = nxt("pp", 2)
            for j, wb in enumerate(wvs):
                w = wview(wb)
                for c in range(NCH):
                    S.op("pe", lambda e, pb=pb, c=c, r0=r0, nr=nr, w=w, j=j: e.matmul(
                        pp[0:nr, pb, j * 128:(j + 1) * 128], hT[:, c, r0:r0 + nr], w[:, c, :],
                        start=(c == 0), stop=(c == NCH - 1)),
                        reads=[("wp", wb)] + hT_all, writes=[("pp", pb)])
            evac(ev_eng(), (lambda slot=slot, nr=nr: V[0:nr, slot, 0:cfg.dv]),
                 (lambda pb=pb, nr=nr: pp[0:nr, pb, 0:cfg.dv]), [("pp", pb)], [("V", parv)])
            yield
        wzs = []
        for j in range(cfg.nzb):
            wzs.append(load_w(b)); b += 1
        for ti in range(NOWN + 1):
            off, _, nr = own_tok(ti)
            pb = nxt("pp", 2)
            for j, wb in enumerate(wzs):
                w = wview(wb)
                for c in range(NCH):
                    S.op("pe", lambda e, pb=pb, c=c, off=off, nr=nr, w=w, j=j: e.matmul(
                        pp[0:nr, pb, j * 128:(j + 1) * 128], hT[:, c, off:off + nr], w[:, c, :],
                        start=(c == 0), stop=(c == NCH - 1)),
                        reads=[("wp", wb)] + hT_all, writes=[("pp", pb)])
            S.op("act", lambda e, pb=pb, nr=nr, ti=ti: e.activation(
                sz[0:nr, ti, :], pp[0:nr, pb, 0:cfg.zw], AF.Silu),
                reads=[("pp", pb)], writes=[("sz", parv)])

        yield

    def attn_unit(u, par, parv, filler):
        qT, kT = UBQ[par]
        V, sz = UBV[parv]
        class G:
            pass

        def make_group(qap_fn, nq, tiles, bias, act_scale, tkey, meta_k, pv_spec, first, last_,
                       C=None, D=None, after_A=None):
            g = G()
            st = {}
            W = len(tiles) * nq

            def A():
                sb = nxt("sc", 4)
                st["sb"] = sb
                for i, (kfn, _) in enumerate(tiles):
                    S.op("pe", lambda e, i=i, kfn=kfn: e.matmul(
                        sc[:, sb, i * nq:(i + 1) * nq], kfn(), qap_fn(), start=True, stop=True),
                        reads=[("qT", par), ("kT", par)], writes=[("sc", sb)])
                if meta_k is not None:
                    S.op("pe", lambda e: e.matmul(sc[0:NMETA, sb, W:W + nq], meta_k(), qap_fn(),
                                                  start=True, stop=True),
                         reads=[("qT", par), ("kT", par)], writes=[("sc", sb)])
                if tiles:
                    if bias is not None:
                        S.op("dve", lambda e: bias(e, tt[:, sb, 0:W], sc[:, sb, 0:W]),
                             reads=[("sc", sb), tkey], writes=[("tt", sb)])
                        S.op("act", lambda e: e.activation(PT[:, sb, 0:W], tt[:, sb, 0:W], AF.Exp,
                                                           scale=act_scale),
                             reads=[("tt", sb)], writes=[("PT", sb)])
                    else:
                        S.op("act", lambda e: e.activation(PT[:, sb, 0:W], sc[:, sb, 0:W], AF.Exp,
                                                           scale=SCALE),
                             reads=[("sc", sb)], writes=[("PT", sb)])
                if meta_k is not None:
                    S.op("act", lambda e: e.activation(PT[0:NMETA, sb, W:W + nq], sc[0:NMETA, sb, W:W + nq],
                                                       AF.Exp, scale=SCALE),
                         reads=[("sc", sb)], writes=[("PT", sb)])
                if after_A is not None:
                    after_A()

            def B():
                sb = st["sb"]
                ops = [("PT", PT, i * nq, slot, 128) for i, (_, slot) in enumerate(tiles)]
                if meta_k is not None:
                    ops.append(("PT", PT, W, cfg.nwin, NMETA))
                for qb, (pvb, nqb) in enumerate(pv_spec):
                    for j, (nm, buf, c0, slot, nk) in enumerate(ops):
                        S.op("pe", lambda e, qb=qb, pvb=pvb, nqb=nqb, buf=buf, c0=c0, slot=slot, nk=nk, j=j:
                             e.matmul(pv[0:nqb, pvb, 0:cfg.dv + 1],
                                      buf[0:nk, sb, c0 + qb * 128:c0 + qb * 128 + nqb],
                                      V[0:nk, slot, 0:cfg.dv + 1],
                                      start=(first and j == 0), stop=(last_ and j == len(ops) - 1)),
                             reads=[(nm, sb), ("V", parv), ("Vones", parv)], writes=[("pv", pvb)])

            g.A, g.B, g.C, g.D = A, B, C, D
            return g

        def run_pipeline(groups, LA=3):
            n = len(groups)
            if n == 0:
                return
            rate = 0.0
            if filler is not None:
                rate = (n_fill + 2) / max(1, n - 3)
            fill_acc = [0.0]
            for i in range(min(LA, n)):
                groups[i].A()
            for i in range(n):
                early = groups[i].C is not None
                if i + LA < n and not early:
                    groups[i + LA].A()
                groups[i].B()
                if groups[i].C is not None:
                    groups[i].C()
                if i + LA < n and early:
                    groups[i + LA].A()
                if i >= 1 and groups[i - 1].D is not None:
                    groups[i - 1].D()
                fill_acc[0] += rate
                while fill_acc[0] >= 1.0:
                    next(filler, None)
                    fill_acc[0] -= 1.0
            if groups[n - 1].D is not None:
                groups[n - 1].D()

        def y_to_yT(yb, col0, nqb, chunk, ooff):
            pb = nxt("pp", 2)
            S.op("pe", lambda e: e.transpose(ppb[:, pb, 0:nqb], ybuf[0:nqb, yb, col0:col0 + 128],
                                             ident[0:nqb, 0:nqb]),
                 reads=[("y", yb), "ident"], writes=[("pp", pb)])
            S.op("act", lambda e: e.copy(yT[:, chunk, ooff:ooff + nqb], ppb[:, pb, 0:nqb]),
                 reads=[("pp", pb)], writes=["yT"])

        def finish_simple(pvb, nqb, ti, unit_chunk, zoff, sink_col=None):
            st = {}
            _, ooff, _ = own_tok(ti)

            def C():
                yb = nxt("y", 2)
                st["yb"] = yb
                if sink_col is None:
                    S.op("dve", lambda e: e.reciprocal(small[0:nqb, 8 + pvb:9 + pvb],
                                                       pv[0:nqb, pvb, cfg.dv:cfg.dv + 1]),
                         reads=[("pv", pvb)], writes=[("rinv", pvb)])
                else:
                    S.op("dve", lambda e: e.tensor_tensor(small[0:nqb, 8 + pvb:9 + pvb],
                                                          pv[0:nqb, pvb, cfg.dv:cfg.dv + 1],
                                                          esink[0:nqb, sink_col:sink_col + 1], ALU.add),
                         reads=[("pv", pvb), "esink"], writes=[("rinv", pvb)])
                    S.op("dve", lambda e: e.reciprocal(small[0:nqb, 8 + pvb:9 + pvb],
                                                       small[0:nqb, 8 + pvb:9 + pvb]),
                         reads=[("rinv", pvb)], writes=[("rinv", pvb)])
                S.op("dve", lambda e: e.scalar_tensor_tensor(
                    ybuf[0:nqb, yb, 0:128], pv[0:nqb, pvb, 0:128], small[0:nqb, 8 + pvb:9 + pvb],
                    sz[0:nqb, ti, zoff:zoff + 128], ALU.mult, ALU.mult),
                    reads=[("pv", pvb), ("rinv", pvb), ("sz", parv)], writes=[("y", yb)])

            def Dd():
                y_to_yT(st["yb"], 0, nqb, unit_chunk, ooff)

            return C, Dd


        meta_k = lambda m: (lambda m=m: kT[:, m, cfg.meta_off:cfg.meta_off + NMETA])
        groups = []
        if kind in (0, 2):
            for n in range(NOWN):
                dls = cfg.dlist[n]
                slots = [cfg.key_slot(n, dl) for dl in dls]
                nt = len(dls)
                for m in range(cfg.nq):
                    qfn = lambda n=n, m=m: qT[:, m, n * 128:(n + 1) * 128]
                    tiles = [((lambda sl=sl: kT[:, 0, sl * 128:(sl + 1) * 128]), sl) for sl in slots]
                    after_A = None
                    if kind == 0:
                        toff = sum(len(x) for x in cfg.dlist[:n]) * 128

                        def bfn(e, out, scs, toff=toff, nt=nt):
                            return e.scalar_tensor_tensor(out, scs, SCALE,
                                                          tbl_bf[:, toff:toff + nt * 128],
                                                          ALU.mult, ALU.add)
                        act_scale = 1.0
                        tkey = ("tbl", n // 4)
                        if n % 4 == 3 and u + 1 < cfg.nunits:
                            def after_A(h=n // 4, u=u):
                                load_na_tbl(u + 1, h)
                    else:
                        var = 0 if n == 0 else (2 if n == NOWN - 1 else 1)
                        slope = alibi_slopes(16)[u * 4 + m]

                        def bfn(e, out, scs, var=var, slope=slope):
                            return e.scalar_tensor_tensor(out, tbl[:, var * 384:(var + 1) * 384],
                                                          -slope / SCALE, scs, ALU.mult, ALU.add)
                        act_scale = SCALE
                        tkey = "tbl"
                    pvb = nxt("pv", 2)
                    if kind == 0:
                        C_, D_ = finish_simple(pvb, 128, n, u, 0)
                    else:
                        C_, D_ = finish_simple(pvb, 128, n, u * 4 + m, m * 128, sink_col=u * 4 + m)
                    if nt <= 3:
                        parts = [(0, nt)]
                    else:
                        parts = [(0, 4), (4, nt)]
                    for pi, (a_, b_) in enumerate(parts):
                        lastp = (pi == len(parts) - 1)
                        if kind == 0:
                            def bfp(e, out, scs, toff=toff, a_=a_, b_=b_):
                                return e.scalar_tensor_tensor(
                                    out, scs, SCALE, tbl_bf[:, toff + a_ * 128:toff + b_ * 128],
                                    ALU.mult, ALU.add)
                        else:
                            bfp = bfn
                        groups.append(make_group(qfn, 128, tiles[a_:b_], bfp, act_scale, tkey,
                                                 meta_k(0) if lastp else None, [(pvb, 128)],
                                                 pi == 0, lastp, C_ if lastp else None,
                                                 D_ if lastp else None, after_A if lastp else None))
            for m in range(cfg.nq):
                qfn = lambda m=m: qT[:, m, NOWN * 128:NOWN * 128 + NMETA]
                pvb = nxt("pv", 2)
                if kind == 0:
                    C_, D_ = finish_simple(pvb, NMETA, NOWN, u, 0)
                else:
                    C_, D_ = finish_simple(pvb, NMETA, NOWN, u * 4 + m, m * 128, sink_col=u * 4 + m)
                groups.append(make_group(qfn, NMETA, [], None, None, None, meta_k(0),
                                         [(pvb, NMETA)], True, True, C_, D_))
        else:
            slope = alibi_slopes(8)[u]
            C0 = 896
            chunks = [(n0 * 128, 256, [n0, n0 + 1]) for n0 in (0, 2, 4, 6)] + \
                     [(NOWN * 128, NMETA, [NOWN])]
            for (qoff, nq, tis) in chunks:
                ob = nxt("x", 2)
                nqb = min(nq, 128)
                pvl = [(i, nqb) for i in range(len(tis))]
                for m in range(2):
                    qfn = lambda m=m, qoff=qoff, nq=nq: qT[:, m, qoff:qoff + nq]
                    for g4 in range(8):
                        slots = [2 * g4 + 1 - i for i in range(2)]
                        tiles = [((lambda sl=sl, m=m: kT[:, m, sl * 128:(sl + 1) * 128]), sl)
                                 for sl in slots]
                        bfn = None
                        if nq == 256:
                            n0 = tis[0]
                            region = 0 if slots[0] < NOWN else 1
                            ks0 = slots[0] if slots[0] < NOWN else slots[0] - NOWN
                            x0 = region * 1920 + 128 * (n0 - ks0) + C0
                            t0_ = tbl[:, x0:x0 + 256]
                            win = bass.AP(t0_.tensor, t0_.offset, [list(t0_.ap[0]), [128, 2], [1, 256]])

                            def bfn(e, out, scs, win=win, slope=slope):
                                return e.scalar_tensor_tensor(
                                    out.rearrange("p (a b) -> p a b", a=2), win, -slope / SCALE,
                                    scs.rearrange("p (a b) -> p a b", a=2), ALU.mult, ALU.add)
                        groups.append(make_group(qfn, nq, tiles, bfn, SCALE, "tbl", None, pvl,
                                                 g4 == 0, False))

                    def C_(m=m, ob=ob, pvl=pvl, tis=tis):
                        for qi, (pvb, nqb_) in enumerate(pvl):
                            S.op("dve", lambda e, pvb=pvb, nqb_=nqb_: e.reciprocal(
                                small[0:nqb_, 8 + pvb:9 + pvb], pv[0:nqb_, pvb, 256:257]),
                                reads=[("pv", pvb)], writes=[("rinv", pvb)])
                            if m == 0:
                                S.op("dve", lambda e, pvb=pvb, nqb_=nqb_, qi=qi: e.tensor_scalar(
                                    obuf[0:nqb_, ob, qi, :], pv[0:nqb_, pvb, 0:256],
                                    small[0:nqb_, 8 + pvb:9 + pvb], None, ALU.mult),
                                    reads=[("pv", pvb), ("rinv", pvb)], writes=[("obuf", ob, qi)])
                            else:
                                S.op("dve", lambda e, pvb=pvb, nqb_=nqb_: e.tensor_scalar(
                                    junk[0:nqb_, :], pv[0:nqb_, pvb, 0:256],
                                    small[0:nqb_, 8 + pvb:9 + pvb], small[0:nqb_, 0:1],
                                    ALU.mult, ALU.mult),
                                    reads=[("pv", pvb), ("rinv", pvb), "small"], writes=["junk"])
                                S.op("dve", lambda e, nqb_=nqb_, qi=qi: e.tensor_tensor(
                                    obuf[0:nqb_, ob, qi, :], obuf[0:nqb_, ob, qi, :], junk[0:nqb_, :],
                                    ALU.add),
                                    reads=["junk", ("obuf", ob, qi)], writes=[("obuf", ob, qi)])
                        if m == 0:
                            return
                        for qi, ti in enumerate(tis):
                            _, ooff, nr = own_tok(ti)
                            o_ap = lambda nr=nr, qi=qi: obuf[0:nr, ob, qi, :]
                            S.op("dve", lambda e, nr=nr, o_ap=o_ap: e.tensor_tensor(
                                junk[0:nr, :], o_ap(), o_ap(), ALU.mult),
                                reads=[("obuf", ob, qi)], writes=["junk"])
                            S.op("dve", lambda e, nr=nr: e.tensor_reduce(
                                small[0:nr, 4:5], junk[0:nr, :], AX.X, ALU.add),
                                reads=["junk"], writes=["ms"])
                            S.op("dve", lambda e, nr=nr: e.tensor_scalar(
                                small[0:nr, 4:5], small[0:nr, 4:5], 1.0 / 256, RMS_EPS,
                                ALU.mult, ALU.add), reads=["ms"], writes=["ms"])
                            S.op("act", lambda e, nr=nr: e.activation(small[0:nr, 4:5],
                                                                      small[0:nr, 4:5], AF.Ln),
                                 reads=["ms"], writes=["ms"])
                            S.op("act", lambda e, nr=nr: e.activation(small[0:nr, 4:5],
                                                                      small[0:nr, 4:5], AF.Exp,
                                                                      scale=-0.5),
                                 reads=["ms"], writes=["ms"])
                            S.op("dve", lambda e, nr=nr, o_ap=o_ap: e.scalar_tensor_tensor(
                                junk[0:nr, :], o_ap(), small[0:nr, 4:5], gl[0:nr, :],
                                ALU.mult, ALU.mult),
                                reads=[("obuf", ob, qi), "ms", "gl"], writes=["junk"])
                            S.op("dve", lambda e, nr=nr, ti=ti, qi=qi: e.tensor_tensor(
                                ybuf[0:nr, qi, :], junk[0:nr, :], sz[0:nr, ti, :], ALU.mult),
                                reads=["junk", ("sz", parv)], writes=[("y", qi)])

                    D_ = None
                    if m == 1:
                        def D_(tis=tis):
                            for qi, ti in enumerate(tis):
                                _, ooff, nr = own_tok(ti)
                                for j in range(2):
                                    y_to_yT(qi, j * 128, nr, 2 * u + j, ooff)
                    groups.append(make_group(qfn, nq, [], None, None, None, meta_k(m), pvl,
                                             False, True, C_, D_))
        run_pipeline(groups)

    def exhaust(gen):
        if gen is not None:
            for _ in gen:
                pass

    if kind == 0:
        load_na_tbl(0, 0)
        load_na_tbl(0, 1)
    def chain(*gens):
        for g_ in gens:
            for _ in g_:
                yield

    if NPAR == 2:
        n_fill_ = n_fill
        exhaust(chain(proj_qk(0, 0), proj_vz(0, 0)))
        for u in range(cfg.nunits):
            p1 = (u + 1) % 2
            filler = chain(proj_qk(u + 1, p1), proj_vz(u + 1, p1)) if u + 1 < cfg.nunits else None
            attn_unit(u, u % 2, u % 2, filler)
            exhaust(filler)
    else:
        exhaust(proj_vz(0, 0))
        for u in range(cfg.nunits):
            exhaust(proj_qk(u, 0))
            filler = proj_vz(u + 1, (u + 1) % 2) if u + 1 < cfg.nunits else None
            attn_unit(u, 0, u % 2, filler)
            exhaust(filler)

    wo_sb = hT_f[:, 0:NCH * D].rearrange("p (c n) -> p c n", c=NCH)
    for nb in range(4):
        S.dma("pool", lambda e, nb=nb: e.dma_start(
            out=wo_sb[:, :, nb * 512:(nb + 1) * 512], in_=wo[nb].rearrange("p (c j) -> p c j", c=NCH)),
            writes=[("wo", nb)] + ((hT_all + [(k_, 1) for k_ in ("qT", "kT", "V", "Vones", "sz")]) if nb == 0 else []))
    lng_sb = V_f[:].bitcast(F32)[:, 0:D]
    lnb_sb = sz_f[:].bitcast(F32)[:, 0:D]
    S.dma("sp", lambda e: e.dma_start(out=lng_sb, in_=lng[L:L + 1, :].partition_broadcast(128)),
          writes=[("V", 0), ("Vones", 0)])
    S.dma("sp", lambda e: e.dma_start(out=lnb_sb, in_=lnb[L:L + 1, :].partition_broadcast(128)),
          writes=[("sz", 0)])
    xbs = [tt[:].rearrange("p a b -> p (a b)"),
           wp[:, 0:2, :].rearrange("p a b -> p (a b)").bitcast(F32)]
    xkeys = [[("tt", 0), ("tt", 1), ("tt", 2), ("tt", 3)], [("wp", 0), ("wp", 0), ("wp", 1), ("wp", 1)]]
    rbs = [qT_f[:].bitcast(F32)[:, 0:D], kT_f[:].bitcast(F32)[:, 0:D]]
    rkeys = [("qT", 0), ("kT", 0)]
    accs = [[sc[:, nb, :] for nb in range(4)], [pp[:, 0, :], pp[:, 1, :], pv[:, 0, :], pv[:, 1, :]]]
    akeys = [[("sc", nb) for nb in range(4)], [("pp", 0), ("pp", 1), ("pv", 0), ("pv", 1)]]
    junkb = PT[:].rearrange("p a b -> p (a b)")
    jkeys = [("PT", k_) for k_ in range(4)]
    if not last:
        xchunks, _ = exch_plan(KINDS[L + 1])
    for ti in range(NOWN + 1):
        off, ooff, nr = own_tok(ti)
        i = ti % 2
        xb, xk, rb, rk, acc, ak = xbs[i], xkeys[i], rbs[i], rkeys[i], accs[i], akeys[i]
        xk_u = list(dict.fromkeys(xk))
        if L == 0:
            S.dma("sp", lambda e, rb=rb, off=off, nr=nr: e.dma_start(
                out=rb[0:nr, :], in_=hwin0[off:off + nr, :]), writes=[rk])
        else:
            S.dma("sp", lambda e, rb=rb, ooff=ooff, nr=nr: e.dma_start(
                out=rb[0:nr, :], in_=hcur[L][ooff:ooff + nr, :]),
                reads=[("hcur", L, ti)], writes=[rk])
        order = ([(c, nb) for nb in range(4) for c in range(NCH)] if ti < 2 else
                 [(c, nb) for c in range(NCH) for nb in range(4)])
        for (c, nb) in order:
            S.op("pe", lambda e, c=c, nb=nb, acc=acc, ooff=ooff, nr=nr: e.matmul(
                acc[nb][0:nr, :], yT[:, c, ooff:ooff + nr], wo_sb[:, c, nb * 512:(nb + 1) * 512],
                start=(c == 0), stop=(c == NCH - 1)),
                reads=[("wo", nb), "yT"], writes=[ak[nb]])
        for nb in range(4):
            S.op("dve", lambda e, nb=nb, xb=xb, rb=rb, acc=acc, nr=nr: e.scalar_tensor_tensor(
                xb[0:nr, nb * 512:(nb + 1) * 512], rb[0:nr, nb * 512:(nb + 1) * 512], ALPHA,
                acc[nb][0:nr, :], ALU.mult, ALU.add),
                reads=[rk, ak[nb]], writes=[xk[nb]])
        S.op("dve", lambda e, xb=xb, nr=nr: e.tensor_reduce(small[0:nr, 16:17], xb[0:nr, :],
                                                            AX.X, ALU.add),
             reads=xk_u, writes=["lnm"])
        S.op("dve", lambda e, nr=nr: e.tensor_scalar(small[0:nr, 16:17], small[0:nr, 16:17],
                                                     -1.0 / D, None, ALU.mult),
             reads=["lnm"], writes=["lnm"])
        S.op("act", lambda e, xb=xb, nr=nr: e.activation(
            xb[0:nr, :], xb[0:nr, :], AF.Identity, bias=small[0:nr, 16:17]),
            reads=xk_u + ["lnm"], writes=xk_u)
        S.op("dve", lambda e: e.memset(small[:, 17:18], 0.0), writes=["lnv"])
        S.op("act", lambda e, xb=xb, nr=nr: e.activation(
            junkb[0:nr, :], xb[0:nr, :], AF.Square, accum_out=small[0:nr, 17:18]),
            reads=xk_u, writes=jkeys + ["lnv"])
        S.op("dve", lambda e, nr=nr: e.tensor_scalar(small[0:nr, 17:18], small[0:nr, 17:18],
                                                     1.0 / D, LN_EPS, ALU.mult, ALU.add),
             reads=["lnv"], writes=["lnv"])
        S.op("act", lambda e, nr=nr: e.activation(small[0:nr, 17:18], small[0:nr, 17:18], AF.Ln),
             reads=["lnv"], writes=["lnv"])
        S.op("act", lambda e, nr=nr: e.activation(small[0:nr, 17:18], small[0:nr, 17:18], AF.Exp,
                                                  scale=-0.5),
             reads=["lnv"], writes=["lnv"])
        S.op("dve", lambda e, xb=xb, nr=nr: e.scalar_tensor_tensor(
            xb[0:nr, :], xb[0:nr, :], small[0:nr, 17:18], lng_sb[0:nr, :], ALU.mult, ALU.mult),
            reads=xk_u + ["lnv", ("V", 0)], writes=xk_u)
        S.op("dve", lambda e, xb=xb, nr=nr: e.tensor_tensor(
            xb[0:nr, :], xb[0:nr, :], lnb_sb[0:nr, :], ALU.add),
            reads=xk_u + [("sz", 0)], writes=xk_u)
        if last:
            if ti < NOWN:
                S.dma("sp", lambda e, xb=xb, ooff=ooff, nr=nr: e.dma_start(
                    out=hout[ooff:ooff + nr, :], in_=xb[0:nr, :]),
                    reads=xk_u, writes=[("hout", ti)])
        else:
            S.dma("sp", lambda e, xb=xb, ooff=ooff, nr=nr: e.dma_start(
                out=hcur[L + 1][ooff:ooff + nr, :], in_=xb[0:nr, :]),
                reads=xk_u, writes=[("hcur", L + 1, ti)])
            S.dma("pool", lambda e, xb=xb, ooff=ooff, nr=nr: e.dma_start(
                out=hbf[L + 1][ooff:ooff + nr, :], in_=xb[0:nr, :]),
                reads=xk_u, writes=[("hbf", L + 1, ti)])
            for ci, sl in enumerate(xchunks):
                if ti in sl:
                    pos = sl.index(ti)
                    S.dma("pool", lambda e, xb=xb, ci=ci, pos=pos: e.dma_start(
                        out=hbx[(L + 1, ci)][pos * 128:(pos + 1) * 128, :], in_=xb[:, :]),
                        reads=xk_u, writes=[("hbx", L + 1, ci, pos)])


_NC = []


def layer_params(i, inputs):
    kind, j = i % 3, i // 3
    if kind == 0:
        return inputs["w_in_a"][j]
    if kind == 1:
        return inputs["w_in_b"][j]
    return inputs["w_in_c"][j]


def kernel(**inputs):
    inputs = {k: np.asarray(v, dtype=np.float32) for k, v in inputs.items()}
    if not _NC:
        _NC.append(build_fused())
    nc = _NC[0]
    cfgs = [Cfg(KINDS[i], i) for i in range(DEPTH)]
    x = inputs["x"]
    meta = inputs["meta_tokens"]
    shared = {"wo": np.ascontiguousarray(np.stack([prep_w_out(inputs["w_out"][i]) for i in range(DEPTH)])),
              "lng": np.ascontiguousarray(inputs["ln_g"]), "lnb": np.ascontiguousarray(inputs["ln_b"])}
    for i in range(DEPTH):
        shared["wq%d" % i] = prep_w_in(cfgs[i], layer_params(i, inputs))
    shared["lamv"] = np.ascontiguousarray(np.stack(
        [inputs["lam_q1_b"][0], inputs["lam_k1_b"][0], inputs["lam_q2_b"][0], inputs["lam_k2_b"][0]], 0))
    shared["subg"] = np.ascontiguousarray(inputs["subln_g_b"][0][None])
    shared["sink"] = np.ascontiguousarray(inputs["sink_c"][0][None])
    eye = np.eye(128, dtype=np.float32)
    zero = np.zeros((128, 128), np.float32)
    zeros_blk = np.zeros((128, D), np.float32)
    per_s = {}
    for s in range(2):
        d = {"idn": np.ascontiguousarray(np.stack([eye, eye if s == 1 else zero, eye if s == 0 else zero]))}
        for i in range(DEPTH):
            if KINDS[i] == 0:
                d["tb%d" % i] = na_bias_tables(cfgs[i], inputs["rpb_a"][i // 3], s)
            elif KINDS[i] == 1:
                d["tb%d" % i] = diff_dist_tables(s)
            else:
                d["tb%d" % i] = swa_dist_tables(s)
        per_s[s] = d
    maps = []
    for b in range(BATCH):
        for s in range(2):
            blocks = win_blocks(cfgs[0], s)
            rows = [x[b, 128 * kb:128 * (kb + 1)] if kb is not None else zeros_blk for kb in blocks]
            rows.append(meta)
            m = {"hwin0": np.ascontiguousarray(np.concatenate(rows, 0))}
            m.update(shared)
            m.update(per_s[s])
            maps.append(m)
    res = run_bass_kernel_spmd(nc, maps, core_ids=list(range(2 * BATCH)))
    out = np.empty((BATCH, SEQ, D), np.float32)
    for b in range(BATCH):
        for s in range(2):
            out[b, 1024 * s:1024 * (s + 1)] = res.results[2 * b + s]["hout"]
    return out
```

```python
import contextlib
import math
import numpy as np
import concourse.bass as bass
import concourse.mybir as mybir
from concourse.bass_utils import run_bass_kernel_spmd

F32 = mybir.dt.float32
BF16 = mybir.dt.bfloat16
AF = mybir.ActivationFunctionType
ALU = mybir.AluOpType
AX = mybir.AxisListType

D = 2048
NCH = 16
SEQ = 2048
BATCH = 4
NMETA = 16
DEPTH = 4
DH = 128
SCALE = DH ** -0.5
ALPHA = (2 * DEPTH) ** 0.25
LN_EPS = 1e-5
RMS_EPS = 1e-5
NEG = -300.0
DBIG = 1.0e5
NOWN = 8
NTOK_OWN = NOWN * 128 + NMETA

ENGS = ["pe", "act", "dve", "pool", "sp"]
NDMA = 32


class Sched:
    def __init__(self, nc):
        self.nc = nc
        self.q = {e: [] for e in ENGS}
        self.count = {e: 0 for e in ENGS}
        self.seen = {e: {} for e in ENGS}
        self.res = {}
        self.dma_next = {"sp": 0, "pool": 0, "act": 0}
        self.dma_cnt = [0] * NDMA

    def _deps(self, eng, reads, writes):
        need = {}

        def add(tok):
            if tok is None:
                return
            k, v = tok
            if need.get(k, 0) < v:
                need[k] = v

        for key in reads:
            st = self.res.get(key)
            if st:
                add(st["w"])
        for key in writes:
            st = self.res.get(key)
            if st:
                add(st["w"])
                for tok in st["r"].items():
                    add(tok)
        waits = []
        seen = self.seen[eng]
        for k, v in need.items():
            if k == eng and eng == "pe":
                continue
            if seen.get(k, 0) >= v:
                continue
            seen[k] = v
            waits.append((k, v))
        return waits

    def _mark(self, tok, reads, writes):
        k, v = tok
        for key in reads:
            st = self.res.setdefault(key, {"w": None, "r": {}})
            if st["r"].get(k, 0) < v:
                st["r"][k] = v
        for key in writes:
            self.res[key] = {"w": tok, "r": {}}

    def op(self, eng, fn, reads=(), writes=()):
        waits = self._deps(eng, reads, writes)
        self.count[eng] += 1
        idx = self.count[eng]
        self.q[eng].append(("op", fn, waits, idx))
        self._mark((eng, idx), reads, writes)
        return (eng, idx)

    def dma(self, eng, fn, reads=(), writes=()):
        half = NDMA // 2
        base = 0 if eng == "sp" else half
        slot = base + self.dma_next[eng]
        self.dma_next[eng] = (self.dma_next[eng] + 1) % half
        waits = self._deps(eng, reads, writes)
        if self.dma_cnt[slot] > 0:
            k, v = ("q%d" % slot, self.dma_cnt[slot])
            if self.seen[eng].get(k, 0) < v:
                self.seen[eng][k] = v
                waits.append((k, v))
        self.dma_cnt[slot] += 1
        tok = ("q%d" % slot, self.dma_cnt[slot])
        self.q[eng].append(("dma", fn, waits, slot))
        self._mark(tok, reads, writes)
        return tok

    def cc(self, fn, reads=(), writes=()):
        waits = self._deps("pool", reads, writes)
        self.cc_cnt = getattr(self, "cc_cnt", 0) + 1
        tok = ("cc", self.cc_cnt)
        self.q["pool"].append(("cc", fn, waits, None))
        self._mark(tok, reads, writes)
        return tok

    def barrier(self):
        for eng in ENGS:
            waits = []
            seen = self.seen[eng]
            allk = [(e, self.count[e]) for e in ENGS[:4]]
            allk += [("q%d" % i, self.dma_cnt[i]) for i in range(NDMA)]
            allk += [("cc", getattr(self, "cc_cnt", 0))]
            for k, v in allk:
                if v > 0 and seen.get(k, 0) < v:
                    seen[k] = v
                    waits.append((k, v))
            self.q[eng].append(("wait", None, waits, None))

    def wait_all(self, eng, keys):
        waits = self._deps(eng, keys, ())
        self.q[eng].append(("wait", None, waits, None))

    def emit(self):
        nc = self.nc
        with contextlib.ExitStack() as es:
            sems = {}
            for e in ENGS[:4]:
                sems[e] = es.enter_context(nc.semaphore("s_" + e))
            for i in range(NDMA):
                sems["q%d" % i] = es.enter_context(nc.semaphore("s_q%d" % i))
            sems["cc"] = es.enter_context(nc.semaphore("s_cc"))
            block = es.enter_context(nc.Block())

            def run(engname):
                def body(engine):
                    for kind, fn, waits, info in self.q[engname]:
                        for k, v in waits:
                            engine.wait_ge(sems[k], v * 16 if k.startswith("q") else v)
                        if kind == "op":
                            fn(engine).then_inc(sems[engname], 1)
                        elif kind == "dma":
                            fn(engine).then_inc(sems["q%d" % info], 16)
                        elif kind == "cc":
                            fn(engine).then_inc(sems["cc"], 1)
                return body

            block.tensor(run("pe"))
            block.scalar(run("act"))
            block.vector(run("dve"))
            block.gpsimd(run("pool"))
            block.sync(run("sp"))


def alibi_slopes(n):
    return [2.0 ** (-8.0 * (i + 1) / n) for i in range(n)]


class Cfg:
    def __init__(self, kind, layer_idx):
        self.kind = kind
        self.layer_idx = layer_idx
        if kind == 0:
            self.nunits, self.nq, self.nk, self.dv, self.zw = 16, 1, 1, 128, 128
            self.nhalo = 2
            self.dlist = [list(range(-2, 4))] + [list(range(-2, 3))] * 6 + [list(range(-3, 3))]
        elif kind == 1:
            self.nunits, self.nq, self.nk, self.dv, self.zw = 8, 2, 2, 256, 256
            self.nhalo = 8
            self.lambda_init = 0.8 - 0.6 * math.exp(-0.3 * layer_idx)
        else:
            self.nunits, self.nq, self.nk, self.dv, self.zw = 4, 4, 1, 128, 512
            self.nhalo = 2
            self.dlist = [[-1, 0, 1]] * 8
        self.nwin = NOWN + self.nhalo
        self.nwt = self.nwin * 128 + NMETA
        self.meta_off = self.nwin * 128
        self.nvb = self.dv // 128
        self.nzb = self.zw // 128
        self.blk_per_unit = self.nq + self.nk + self.nvb + self.nzb
        self.nblk = self.blk_per_unit * self.nunits

    def key_slot(self, n, dl):
        k = n + dl
        if 0 <= k < NOWN:
            return k
        if self.kind == 0:
            return {-1: 8, -2: 9, 8: 8, 9: 9}[k]
        return {-1: 8, 8: 9}[k]


def win_blocks(cfg, s):
    lo = NOWN * s
    own = list(range(lo, lo + NOWN))
    if cfg.kind == 1:
        other = list(range(NOWN * (1 - s), NOWN * (1 - s) + NOWN))
        return own + other
    if cfg.kind == 0:
        halo = [lo + NOWN, lo + NOWN + 1] if s == 0 else [lo - 1, lo - 2]
    else:
        halo = [lo - 1, lo + NOWN]
    return own + [b if 0 <= b < 16 else None for b in halo]


def unit_cols(cfg, u):
    if cfg.kind == 0:
        return [u * 128, 2048 + u * 128, 4096 + u * 128, 6144 + u * 128]
    if cfg.kind == 1:
        b = u * 256
        return [4096 + b, 4096 + b + 128, 6144 + b, 6144 + b + 128,
                b, b + 128, 2048 + b, 2048 + b + 128]
    q = [u * 512 + g * 128 for g in range(4)]
    z = [3072 + u * 512 + g * 128 for g in range(4)]
    return q + [2048 + u * 128, 2560 + u * 128] + z


def prep_w_in(cfg, w):
    cols = []
    for u in range(cfg.nunits):
        cols += unit_cols(cfg, u)
    idx = (np.asarray(cols)[:, None] + np.arange(128)[None]).reshape(-1)
    wg = np.ascontiguousarray(w[:, idx])
    wg = wg.reshape(NCH, 128, len(cols), 128).transpose(2, 1, 0, 3)
    return np.ascontiguousarray(wg).reshape(len(cols), 128, NCH * 128)


def prep_w_out(w):
    wg = w.reshape(NCH, 128, 4, 512).transpose(2, 1, 0, 3)
    return np.ascontiguousarray(wg).reshape(4, 128, NCH * 512)


def na_bias_tables(cfg, rpb, s):
    rows = 32
    kr_l = np.arange(128) // 64
    kc = np.arange(128) % 64
    qr_l = np.arange(128) // 64
    qc = np.arange(128) % 64
    cs = np.clip(qc - 8, 0, 64 - 16)
    colok = (kc[:, None] >= cs[None, :]) & (kc[:, None] < cs[None, :] + 16)
    dc = kc[:, None] - qc[None, :] + 15
    dc_c = np.clip(dc, 0, 30)
    tiles = []
    for n in range(NOWN):
        qb = NOWN * s + n
        qr = 2 * qb + qr_l
        rs = np.clip(qr - 4, 0, rows - 8)
        for dl in cfg.dlist[n]:
            kb = qb + dl
            if kb < 0 or kb > 15:
                tiles.append(np.full((16, 128, 128), NEG, np.float32))
                continue
            kr = 2 * kb + kr_l
            rowok = (kr[:, None] >= rs[None, :]) & (kr[:, None] < rs[None, :] + 8)
            dr = kr[:, None] - qr[None, :] + 7
            dr_c = np.clip(dr, 0, 14)
            vals = rpb[:, dr_c, dc_c]
            ok = (rowok & colok)[None]
            tiles.append(np.where(ok, vals, np.float32(NEG)).astype(np.float32))
    return np.ascontiguousarray(np.concatenate(tiles, axis=2))


def diff_dist_tables(s):
    C = 896
    x = np.arange(1920)[None, :]
    p = np.arange(128)[:, None]
    own = np.abs(x - C - p)
    sg = -1 if s == 0 else 1
    oth = 1024 + sg * (x - C - p)
    return np.concatenate([own, oth], axis=1).astype(np.float32)


def swa_dist_tables(s):
    p = np.arange(128)[:, None]
    q = np.arange(128)[None, :]

    def tile(dl, valid=True):
        u = q - (128 * dl + p)
        d = np.abs(u).astype(np.float32)
        d = np.where(d <= 128, d, np.float32(DBIG))
        if not valid:
            d = np.full_like(d, DBIG)
        return d

    first = [tile(-1, s == 1), tile(0), tile(1)]
    mid = [tile(-1), tile(0), tile(1)]
    last = [tile(-1), tile(0), tile(1, s == 0)]
    return np.concatenate(first + mid + last, axis=1).astype(np.float32)


KINDS = [i % 3 for i in range(DEPTH)]
PAIRS = [[0, 1], [2, 3], [4, 5], [6, 7]]


def exch_plan(kind):
    if kind == 1:
        chunks = [[0, 1, 2, 3], [4, 5, 6, 7]]
        halo = {}
        for j in range(NOWN):
            ch, k = j // 4, j % 4
            halo[NOWN + j] = [(ch, k * 128, "A"), (ch, 512 + k * 128, "B")]
        return chunks, halo
    if kind == 0:
        chunks = [[6, 7, 0, 1]]
        halo = {8: [(0, 128, "A"), (0, 512 + 256, "B")], 9: [(0, 0, "A"), (0, 512 + 384, "B")]}
        return chunks, halo
    chunks = [[7, 0]]
    halo = {8: [(0, 0, "A")], 9: [(0, 256 + 128, "B")]}
    return chunks, halo


def build_fused():
    nc = bass.Bass("TRN2", target_bir_lowering=False)
    cfgs = [Cfg(KINDS[i], i) for i in range(DEPTH)]
    dt = lambda name, shape, k="ExternalInput", d=F32: nc.dram_tensor(name, shape, d, kind=k).ap()
    hwin0 = dt("hwin0", [cfgs[0].nwt, D])
    wqs = [dt("wq%d" % i, [cfgs[i].nblk, 128, NCH * 128]) for i in range(DEPTH)]
    wos = dt("wo", [DEPTH, 4, 128, NCH * 512])
    lng = dt("lng", [DEPTH, D])
    lnb = dt("lnb", [DEPTH, D])
    idn = dt("idn", [3, 128, 128])
    hout = dt("hout", [NOWN * 128, D], "ExternalOutput")
    pre = dt("pre", [NTOK_OWN, D], "Internal")
    tbs = {}
    for i in range(DEPTH):
        if KINDS[i] == 0:
            nt_ = sum(len(x) for x in cfgs[i].dlist)
            tbs[i] = dt("tb%d" % i, [16, 128, nt_ * 128])
        elif KINDS[i] == 1:
            tbs[i] = dt("tb%d" % i, [128, 2 * 1920])
        else:
            tbs[i] = dt("tb%d" % i, [128, 9 * 128])
    lamv = dt("lamv", [4, 128])
    subg = dt("subg", [1, 256])
    sink = dt("sink", [1, 16])
    hcur = {L: dt("hcur%d" % L, [NTOK_OWN, D], "Internal") for L in range(1, DEPTH)}
    hbf = {L: dt("hbf%d" % L, [NTOK_OWN, D], "Internal", BF16) for L in range(1, DEPTH)}
    hbx, hgt = {}, {}
    for L in range(1, DEPTH):
        chunks, _ = exch_plan(KINDS[L])
        for ci, sl in enumerate(chunks):
            hbx[(L, ci)] = nc.dram_tensor("hbx%d_%d" % (L, ci), [len(sl) * 128, D], BF16)
            hgt[(L, ci)] = nc.dram_tensor("hgt%d_%d" % (L, ci), [2 * len(sl) * 128, D], BF16)

    NW = 6
    mx = lambda f: max(f(c) for c in cfgs)
    with contextlib.ExitStack() as es:
        T = lambda name, shape, d=F32: es.enter_context(nc.sbuf_tensor(name, shape, d))
        P = lambda name, shape, d=F32: es.enter_context(nc.psum_tensor(name, shape, d))
        hT_f = T("hT", [128, NCH * mx(lambda c: c.nwt)], BF16)
        yT = T("yT", [128, NCH, NTOK_OWN], BF16)
        wp = T("wp", [128, NW, NCH * 128], BF16)
        qT_f = T("qT", [128, mx(lambda c: c.nq) * NTOK_OWN], BF16)
        kT_f = T("kT", [128, mx(lambda c: c.nk * c.nwt)], BF16)
        V_f = T("V", [128, mx(lambda c: (c.nwin + 1) * (c.dv + 2))], BF16)
        sz_f = T("sz", [128, mx(lambda c: (NOWN + 1) * c.zw)], BF16)
        PT = T("PT", [128, 4, 512], BF16)
        PTm = None
        tt = T("tt", [128, 4, 512], F32)
        vz1 = T("vz1", [128, 6700], BF16)
        stg = vz1[:, 0:3 * D].rearrange("p (a b) -> p a b", a=3)
        ident3 = T("ident", [128, 3, 128], BF16)
        ybuf = T("ybuf", [128, 2, 256], BF16)
        small = T("small", [128, 64], F32)
        ez = T("ez", [128, 512], F32)
        NTB = 42 * 128
        tbl_f = T("tbl", [128, 2 * 1920], F32)
        obuf = T("obuf", [128, 2, 2, 256], F32)
        lamt = T("lamt", [128, 4, 128], F32)
        gl = T("gl", [128, 256], F32)
        junk = T("junk", [128, 256], F32)
        esink = T("esink", [128, 16], F32)
        rbuf = qT_f[:].bitcast(F32)[:, 0:1024].rearrange("p (a b) -> p a b", a=2)
        rbuf2 = kT_f[:].bitcast(F32)[:, 0:1024].rearrange("p (a b) -> p a b", a=2)

        pp = P("pp", [128, 2, 512], F32)
        sc = P("sc", [128, 4, 512], F32)
        pv = P("pv", [128, 2, 512], F32)
        ppb = pp[:].bitcast(BF16)
        ident = ident3[:, 0, :]
        maskid = {"A": ident3[:, 1, :], "B": ident3[:, 2, :]}

        S = Sched(nc)
        cnt = {"pp": 0, "sc": 0, "pv": 0, "wp": 0, "stg": 0, "y": 0, "x": 0, "r": 0}

        def nxt(name, n):
            v = cnt[name] % n
            cnt[name] += 1
            return v

        S.dma("pool", lambda e: e.dma_start(out=ident3[:], in_=idn.rearrange("a p j -> p a j")),
              writes=["ident"])

        for L in range(DEPTH):
            emit_layer(nc, S, L, cfgs[L], locals())
            if L < DEPTH - 1:
                S.barrier()
        S.wait_all("sp", [("hout", ti) for ti in range(NOWN)])
        S.emit()
    return nc


def emit_layer(nc, S, L, cfg, E):
    kind = cfg.kind
    NWT = cfg.nwt
    nxt = E["nxt"]
    hT_f, yT, wp, qT_f, kT_f, V_f, sz_f = E["hT_f"], E["yT"], E["wp"], E["qT_f"], E["kT_f"], E["V_f"], E["sz_f"]
    PT, PTm, tt, stg, ident, maskid, ybuf, small = (E["PT"], E["PTm"], E["tt"], E["stg"], E["ident"],
                                                   E["maskid"], E["ybuf"], E["small"])
    tbl_f, obuf, lamt, gl, junk, esink, rbuf, rbuf2 = (E["tbl_f"], E["obuf"], E["lamt"], E["gl"], E["junk"],
                                                       E["esink"], E["rbuf"], E["rbuf2"])
    vz1 = E["vz1"]
    ez = E["ez"]
    pp, sc, pv, ppb = E["pp"], E["sc"], E["pv"], E["ppb"]
    wq, wo, tb = E["wqs"][L], E["wos"][L], E["tbs"][L]
    lng, lnb, pre, hout = E["lng"], E["lnb"], E["pre"], E["hout"]
    lamv, subg, sink = E["lamv"], E["subg"], E["sink"]
    hwin0, hcur, hbf, hbx, hgt = E["hwin0"], E["hcur"], E["hbf"], E["hbx"], E["hgt"]
    NW = E["NW"]
    NTB = E["NTB"]
    last = (L == DEPTH - 1)

    hT = hT_f[:, 0:NCH * NWT].rearrange("p (c t) -> p c t", c=NCH)
    qT = qT_f[:, 0:cfg.nq * NTOK_OWN].rearrange("p (m t) -> p m t", m=cfg.nq)
    kT = kT_f[:, 0:cfg.nk * NWT].rearrange("p (m t) -> p m t", m=cfg.nk)
    V = V_f[:, 0:(cfg.nwin + 1) * (cfg.dv + 2)].rearrange("p (s d) -> p s d", s=cfg.nwin + 1)
    sz = sz_f[:, 0:(NOWN + 1) * cfg.zw].rearrange("p (t z) -> p t z", t=NOWN + 1)
    UBQ = [(qT, kT)]
    UBV = [(V, sz)]
    if kind != 1:
        o0 = NCH * NWT
        n_q, n_k = cfg.nq * NTOK_OWN, cfg.nk * NWT
        n_v, n_z = (cfg.nwin + 1) * (cfg.dv + 2), (NOWN + 1) * cfg.zw
        assert o0 + n_q + n_k + n_v + n_z <= NCH * 2064
        qT1 = hT_f[:, o0:o0 + n_q].rearrange("p (m t) -> p m t", m=cfg.nq)
        kT1 = hT_f[:, o0 + n_q:o0 + n_q + n_k].rearrange("p (m t) -> p m t", m=cfg.nk)
        V1 = hT_f[:, o0 + n_q + n_k:o0 + n_q + n_k + n_v].rearrange("p (s d) -> p s d", s=cfg.nwin + 1)
        sz1 = hT_f[:, o0 + n_q + n_k + n_v:o0 + n_q + n_k + n_v + n_z].rearrange(
            "p (t z) -> p t z", t=NOWN + 1)
        UBQ.append((qT1, kT1))
        UBV.append((V1, sz1))
    else:
        n_v, n_z = (cfg.nwin + 1) * (cfg.dv + 2), (NOWN + 1) * cfg.zw
        V1 = vz1[:, 0:n_v].rearrange("p (s d) -> p s d", s=cfg.nwin + 1)
        sz1 = vz1[:, n_v:n_v + n_z].rearrange("p (t z) -> p t z", t=NOWN + 1)
        UBV.append((V1, sz1))
    NPAR = len(UBQ)
    tbl = tbl_f
    if kind == 0:
        ntile_tot = sum(len(x) for x in cfg.dlist)
        tbl_bf = tbl_f[:].bitcast(BF16)
        nhalf = sum(len(x) for x in cfg.dlist[:4]) * 128

    if L > 0:
        xch_in, _ = exch_plan(kind)
        for ci, sl in enumerate(xch_in):
            S.cc(lambda e, ci=ci: e.collective_compute(
                "AllGather", ALU.bypass, replica_groups=PAIRS,
                ins=[hbx[(L, ci)].ap().opt()], outs=[hgt[(L, ci)].ap().opt()]),
                reads=[("hbx", L, ci, pos) for pos in range(len(sl))],
                writes=[("hgt", L, ci)])

    if kind == 1:
        S.dma("sp", lambda e: e.dma_start(out=tbl[:, 0:2 * 1920], in_=tb), writes=["tbl"])
        for i in range(4):
            S.dma("sp", lambda e, i=i: e.dma_start(
                out=lamt[:, i, :], in_=lamv[i:i + 1, :].partition_broadcast(128)), writes=["lamt"])
        S.dma("sp", lambda e: e.dma_start(out=gl[:], in_=subg.partition_broadcast(128)),
              writes=["gl"])
        S.op("act", lambda e: e.mul(gl[:], gl[:], 1.0 - cfg.lambda_init), reads=["gl"], writes=["gl"])
        for j in range(2):
            S.op("dve", lambda e, j=j: e.tensor_tensor(junk[:, 0:128], lamt[:, 2 * j, :],
                                                       lamt[:, 2 * j + 1, :], ALU.mult),
                 reads=["lamt"], writes=["junk"])
            S.op("dve", lambda e, j=j: e.tensor_reduce(small[:, 1 + j:2 + j], junk[:, 0:128],
                                                       AX.X, ALU.add),
                 reads=["junk"], writes=["small"])
        S.op("act", lambda e: e.activation(small[:, 1:3], small[:, 1:3], AF.Exp),
             reads=["small"], writes=["small"])
        S.op("dve", lambda e: e.scalar_tensor_tensor(small[:, 0:1], small[:, 2:3],
                                                     -cfg.lambda_init, small[:, 1:2],
                                                     ALU.add, ALU.subtract),
             reads=["small"], writes=["small"])
    elif kind == 2:
        S.dma("sp", lambda e: e.dma_start(out=tbl[:, 0:9 * 128], in_=tb), writes=["tbl"])
        S.dma("sp", lambda e: e.dma_start(out=esink[:], in_=sink.partition_broadcast(128)),
              writes=["esink"])
        S.op("act", lambda e: e.activation(esink[:], esink[:], AF.Exp),
             reads=["esink"], writes=["esink"])

    def tok_rows(slot):
        if slot < cfg.nwin:
            return slot * 128, 128
        return cfg.meta_off, NMETA

    def own_tok(ti):
        if ti < NOWN:
            return ti * 128, ti * 128, 128
        return cfg.meta_off, NOWN * 128, NMETA

    def transpose_rows(slot, sb, nr, rkeys):
        r0, _ = tok_rows(slot)
        for half in range(2):
            pb = nxt("pp", 2)
            for c8 in range(8):
                c = half * 8 + c8
                S.op("pe", lambda e, pb=pb, c8=c8, c=c: e.transpose(
                    ppb[:, pb, c8 * 128:c8 * 128 + nr], stg[0:nr, sb, c * 128:(c + 1) * 128],
                    ident[0:nr, 0:nr]),
                    reads=rkeys + ["ident"], writes=[("pp", pb)])
            src = lambda pb=pb: ppb[:, pb, :].rearrange("p (c t) -> p c t", c=8)[:, :, 0:nr]
            dst = lambda half=half: hT[:, half * 8:half * 8 + 8, r0:r0 + nr]
            if half == 0:
                S.op("dve", lambda e, src=src, dst=dst: e.tensor_copy(dst(), src()),
                     reads=[("pp", pb)], writes=[("hT", slot)])
            else:
                S.op("act", lambda e, src=src, dst=dst: e.copy(dst(), src()),
                     reads=[("pp", pb)], writes=[("hT", slot)])

    if L == 0:
        for slot in range(cfg.nwin + 1):
            r0, nr = tok_rows(slot)
            sb = nxt("stg", 3)
            S.dma("pool", lambda e, sb=sb, r0=r0, nr=nr: e.dma_start(
                out=stg[0:nr, sb, :], in_=hwin0[r0:r0 + nr, :]), writes=[("stg", sb)])
            transpose_rows(slot, sb, nr, [("stg", sb)])
    else:
        _, halo = exch_plan(kind)
        for ti in range(NOWN + 1):
            _, ooff, nr = own_tok(ti)
            slot = ti if ti < NOWN else cfg.nwin
            sb = nxt("stg", 3)
            S.dma("sp", lambda e, sb=sb, ooff=ooff, nr=nr: e.dma_start(
                out=stg[0:nr, sb, :], in_=hbf[L][ooff:ooff + nr, :]),
                reads=[("hbf", L, ti)], writes=[("stg", sb)])
            transpose_rows(slot, sb, nr, [("stg", sb)])
        for slot in sorted(halo):
            cands = halo[slot]
            sbs = []
            for (ci, roff, mk) in cands:
                sb = nxt("stg", 3)
                sbs.append(sb)
                S.dma("sp", lambda e, sb=sb, ci=ci, roff=roff: e.dma_start(
                    out=stg[:, sb, :], in_=hgt[(L, ci)][roff:roff + 128, :]),
                    reads=[("hgt", L, ci)], writes=[("stg", sb)])
            r0 = slot * 128
            for q4 in range(4):
                pb = nxt("pp", 2)
                for c4 in range(4):
                    c = q4 * 4 + c4
                    for k, (ci, roff, mk) in enumerate(cands):
                        S.op("pe", lambda e, pb=pb, c4=c4, c=c, k=k, mk=mk, sb=sbs[k]: e.matmul(
                            pp[:, pb, c4 * 128:(c4 + 1) * 128], stg[:, sb, c * 128:(c + 1) * 128],
                            maskid[mk], start=(k == 0), stop=(k == len(cands) - 1)),
                            reads=[("stg", sbs[k]), "ident"], writes=[("pp", pb)])
                src = lambda pb=pb: pp[:, pb, :].rearrange("p (c t) -> p c t", c=4)
                dst = lambda q4=q4, r0=r0: hT[:, q4 * 4:q4 * 4 + 4, r0:r0 + 128]
                if q4 % 2 == 0:
                    S.op("dve", lambda e, src=src, dst=dst: e.tensor_copy(dst(), src()),
                         reads=[("pp", pb)], writes=[("hT", slot)])
                else:
                    S.op("act", lambda e, src=src, dst=dst: e.copy(dst(), src()),
                         reads=[("pp", pb)], writes=[("hT", slot)])

    hT_all = [("hT", s_) for s_ in range(cfg.nwin + 1)]

    for par_ in range(len(UBV)):
        S.op("pool", lambda e, par_=par_: e.memset(UBV[par_][0][:, :, cfg.dv:cfg.dv + 2], 1.0),
             writes=[("Vones", par_), ("V", par_), ("stg", 0), ("stg", 1), ("stg", 2)])

    wb_of = {}
    w_next = [0]

    def load_w(blk):
        upto = min(blk + 3, cfg.nblk)
        while w_next[0] < upto:
            bi = w_next[0]
            wb = nxt("wp", NW)
            wb_of[bi] = wb
            S.dma("pool", lambda e, wb=wb, bi=bi: e.dma_start(out=wp[:, wb, :], in_=wq[bi]),
                  writes=[("wp", wb)])
            w_next[0] += 1
        return wb_of[blk]

    def wview(wb):
        return wp[:, wb, :].rearrange("p (c j) -> p c j", c=NCH)

    def evac(eng, dst, src, reads, writes):
        if eng == "dve":
            S.op("dve", lambda e: e.tensor_copy(dst(), src()), reads=reads, writes=writes)
        else:
            S.op("act", lambda e: e.copy(dst(), src()), reads=reads, writes=writes)

    ev_rr = [0]

    def ev_eng():
        ev_rr[0] += 1
        return "dve" if ev_rr[0] % 2 else "act"

    def proj_fm(wb, dst_fn, tok_chunks, hkeys, wkey_dst):
        w = wview(wb)
        for (off, doff, n) in tok_chunks:
            pb = nxt("pp", 2)
            for c in range(NCH):
                S.op("pe", lambda e, pb=pb, c=c, off=off, n=n, w=w: e.matmul(
                    pp[:, pb, 0:n], w[:, c, :], hT[:, c, off:off + n],
                    start=(c == 0), stop=(c == NCH - 1)),
                    reads=[("wp", wb)] + hkeys, writes=[("pp", pb)])
            evac(ev_eng(), (lambda doff=doff, n=n: dst_fn(doff, n)),
                 (lambda pb=pb, n=n: pp[:, pb, 0:n]), [("pp", pb)], [wkey_dst])
            yield

    own_chunks = [(0, 0, 512), (512, 512, 512), (cfg.meta_off, NOWN * 128, NMETA)]
    win_chunks = []
    o_ = 0
    while o_ < cfg.nwin * 128:
        n_ = min(512, cfg.nwin * 128 - o_)
        win_chunks.append((o_, o_, n_))
        o_ += n_
    win_chunks.append((cfg.meta_off, cfg.meta_off, NMETA))

    def load_na_tbl(u_, h):
        c0, c1 = (0, nhalf) if h == 0 else (nhalf, ntile_tot * 128)
        S.dma("pool", lambda e: e.dma_start(out=tbl_bf[:, c0:c1], in_=tb[u_][:, c0:c1]),
              writes=[("tbl", h)])

    n_fill = cfg.nwin + 1 + (2 + cfg.nq * len(own_chunks) + cfg.nk * len(win_chunks) if kind != 1 else 0)

    def proj_qk(u, par):
        qT, kT = UBQ[par]
        b = u * cfg.blk_per_unit + (cfg.nvb + cfg.nzb if kind == 1 else 0)
        for m in range(cfg.nq):
            wb = load_w(b); b += 1
            for _ in proj_fm(wb, (lambda doff, n, m=m: qT[:, m, doff:doff + n]), own_chunks, hT_all,
                             ("qT", par)):
                yield
        for m in range(cfg.nk):
            wb = load_w(b); b += 1
            for _ in proj_fm(wb, (lambda doff, n, m=m: kT[:, m, doff:doff + n]), win_chunks, hT_all,
                             ("kT", par)):
                yield
        yield

    def proj_vz(u, parv):
        V, sz = UBV[parv]
        b = u * cfg.blk_per_unit + (0 if kind == 1 else cfg.nq + cfg.nk)
        wvs = []
        for j in range(cfg.nvb):
            wvs.append(load_w(b)); b += 1
        for slot in range(cfg.nwin + 1):
            r0, nr = tok_rows(slot)
            pb = nxt("pp", 2)
            for j, wb in enumerate(wvs):
                w = wview(wb)
                for c in range(NCH):
                    S.op("pe", lambda e, pb=pb, c=c, r0=r0, nr=nr, w=w, j=j: e.matmul(
                        pp[0:nr, pb, j * 128:(j + 1) * 128], hT[:, c, r0:r0 + nr], w[:, c, :],
                        start=(c == 0), stop=(c == NCH - 1)),
                        reads=[("wp", wb)] + hT_all, writes=[("pp", pb)])
            evac(ev_eng(), (lambda slot=slot, nr=nr: V[0:nr, slot, 0:cfg.dv]),
                 (lambda pb=pb, nr=nr: pp[0:nr, pb, 0:cfg.dv]), [("pp", pb)], [("V", parv)])
            yield
        wzs = []
        for j in range(cfg.nzb):
            wzs.append(load_w(b)); b += 1
        for ti in range(NOWN + 1):
            off, _, nr = own_tok(ti)
            pb = nxt("pp", 2)
            for j, wb in enumerate(wzs):
                w = wview(wb)
                for c in range(NCH):
                    S.op("pe", lambda e, pb=pb, c=c, off=off, nr=nr, w=w, j=j: e.matmul(
                        pp[0:nr, pb, j * 128:(j + 1) * 128], hT[:, c, off:off + nr], w[:, c, :],
                        start=(c == 0), stop=(c == NCH - 1)),
                        reads=[("wp", wb)] + hT_all, writes=[("pp", pb)])
            S.op("act", lambda e, pb=pb, nr=nr, ti=ti: e.activation(
                sz[0:nr, ti, :], pp[0:nr, pb, 0:cfg.zw], AF.Silu),
                reads=[("pp", pb)], writes=[("sz", parv)])

        yield

    def attn_unit(u, par, parv, filler):
        qT, kT = UBQ[par]
        V, sz = UBV[parv]
        class G:
            pass

        def make_group(qap_fn, nq, tiles, bias, act_scale, tkey, meta_k, pv_spec, first, last_,
                       C=None, D=None, after_A=None):
            g = G()
            st = {}
            W = len(tiles) * nq

            def A():
                sb = nxt("sc", 4)
                st["sb"] = sb
                for i, (kfn, _) in enumerate(tiles):
                    S.op("pe", lambda e, i=i, kfn=kfn: e.matmul(
                        sc[:, sb, i * nq:(i + 1) * nq], kfn(), qap_fn(), start=True, stop=True),
                        reads=[("qT", par), ("kT", par)], writes=[("sc", sb)])
                if meta_k is not None:
                    S.op("pe", lambda e: e.matmul(sc[0:NMETA, sb, W:W + nq], meta_k(), qap_fn(),
                                                  start=True, stop=True),
                         reads=[("qT", par), ("kT", par)], writes=[("sc", sb)])
                if tiles:
                    if bias is not None:
                        S.op("dve", lambda e: bias(e, tt[:, sb, 0:W], sc[:, sb, 0:W]),
                             reads=[("sc", sb), tkey], writes=[("tt", sb)])
                        S.op("act", lambda e: e.activation(PT[:, sb, 0:W], tt[:, sb, 0:W], AF.Exp,
                                                           scale=act_scale),
                             reads=[("tt", sb)], writes=[("PT", sb)])
                    else:
                        S.op("act", lambda e: e.activation(PT[:, sb, 0:W], sc[:, sb, 0:W], AF.Exp,
                                                           scale=SCALE),
                             reads=[("sc", sb)], writes=[("PT", sb)])
                if meta_k is not None:
                    S.op("act", lambda e: e.activation(PT[0:NMETA, sb, W:W + nq], sc[0:NMETA, sb, W:W + nq],
                                                       AF.Exp, scale=SCALE),
                         reads=[("sc", sb)], writes=[("PT", sb)])
                if after_A is not None:
                    after_A()

            def B():
                sb = st["sb"]
                ops = [("PT", PT, i * nq, slot, 128) for i, (_, slot) in enumerate(tiles)]
                if meta_k is not None:
                    ops.append(("PT", PT, W, cfg.nwin, NMETA))
                for qb, (pvb, nqb) in enumerate(pv_spec):
                    for j, (nm, buf, c0, slot, nk) in enumerate(ops):
                        S.op("pe", lambda e, qb=qb, pvb=pvb, nqb=nqb, buf=buf, c0=c0, slot=slot, nk=nk, j=j:
                             e.matmul(pv[0:nqb, pvb, 0:cfg.dv + 1],
                                      buf[0:nk, sb, c0 + qb * 128:c0 + qb * 128 + nqb],
                                      V[0:nk, slot, 0:cfg.dv + 1],
                                      start=(first and j == 0), stop=(last_ and j == len(ops) - 1)),
                             reads=[(nm, sb), ("V", parv), ("Vones", parv)], writes=[("pv", pvb)])

            g.A, g.B, g.C, g.D = A, B, C, D
            return g

        def run_pipeline(groups, LA=3):
            n = len(groups)
            if n == 0:
                return
            rate = 0.0
            if filler is not None:
                rate = (n_fill + 2) / max(1, n - 3)
            fill_acc = [0.0]
            for i in range(min(LA, n)):
                groups[i].A()
            for i in range(n):
                early = groups[i].C is not None
                if i + LA < n and not early:
                    groups[i + LA].A()
                groups[i].B()
                if groups[i].C is not None:
                    groups[i].C()
                if i + LA < n and early:
                    groups[i + LA].A()
                if i >= 1 and groups[i - 1].D is not None:
                    groups[i - 1].D()
                fill_acc[0] += rate
                while fill_acc[0] >= 1.0:
                    next(filler, None)
                    fill_acc[0] -= 1.0
            if groups[n - 1].D is not None:
                groups[n - 1].D()

        def y_to_yT(yb, col0, nqb, chunk, ooff):
            pb = nxt("pp", 2)
            S.op("pe", lambda e: e.transpose(ppb[:, pb, 0:nqb], ybuf[0:nqb, yb, col0:col0 + 128],
                                             ident[0:nqb, 0:nqb]),
                 reads=[("y", yb), "ident"], writes=[("pp", pb)])
            S.op("act", lambda e: e.copy(yT[:, chunk, ooff:ooff + nqb], ppb[:, pb, 0:nqb]),
                 reads=[("pp", pb)], writes=["yT"])

        def finish_simple(pvb, nqb, ti, unit_chunk, zoff, sink_col=None):
            st = {}
            _, ooff, _ = own_tok(ti)

            def C():
                yb = nxt("y", 2)
                st["yb"] = yb
                if sink_col is None:
                    S.op("dve", lambda e: e.reciprocal(small[0:nqb, 8 + pvb:9 + pvb],
                                                       pv[0:nqb, pvb, cfg.dv:cfg.dv + 1]),
                         reads=[("pv", pvb)], writes=[("rinv", pvb)])
                else:
                    S.op("dve", lambda e: e.tensor_tensor(small[0:nqb, 8 + pvb:9 + pvb],
                                                          pv[0:nqb, pvb, cfg.dv:cfg.dv + 1],
                                                          esink[0:nqb, sink_col:sink_col + 1], ALU.add),
                         reads=[("pv", pvb), "esink"], writes=[("rinv", pvb)])
                    S.op("dve", lambda e: e.reciprocal(small[0:nqb, 8 + pvb:9 + pvb],
                                                       small[0:nqb, 8 + pvb:9 + pvb]),
                         reads=[("rinv", pvb)], writes=[("rinv", pvb)])
                S.op("dve", lambda e: e.scalar_tensor_tensor(
                    ybuf[0:nqb, yb, 0:128], pv[0:nqb, pvb, 0:128], small[0:nqb, 8 + pvb:9 + pvb],
                    sz[0:nqb, ti, zoff:zoff + 128], ALU.mult, ALU.mult),
                    reads=[("pv", pvb), ("rinv", pvb), ("sz", parv)], writes=[("y", yb)])

            def Dd():
                y_to_yT(st["yb"], 0, nqb, unit_chunk, ooff)

            return C, Dd


        meta_k = lambda m: (lambda m=m: kT[:, m, cfg.meta_off:cfg.meta_off + NMETA])
        groups = []
        if kind in (0, 2):
            for n in range(NOWN):
                dls = cfg.dlist[n]
                slots = [cfg.key_slot(n, dl) for dl in dls]
                nt = len(dls)
                for m in range(cfg.nq):
                    qfn = lambda n=n, m=m: qT[:, m, n * 128:(n + 1) * 128]
                    tiles = [((lambda sl=sl: kT[:, 0, sl * 128:(sl + 1) * 128]), sl) for sl in slots]
                    after_A = None
                    if kind == 0:
                        toff = sum(len(x) for x in cfg.dlist[:n]) * 128

                        def bfn(e, out, scs, toff=toff, nt=nt):
                            return e.scalar_tensor_tensor(out, scs, SCALE,
                                                          tbl_bf[:, toff:toff + nt * 128],
                                                          ALU.mult, ALU.add)
                        act_scale = 1.0
                        tkey = ("tbl", n // 4)
                        if n % 4 == 3 and u + 1 < cfg.nunits:
                            def after_A(h=n // 4, u=u):
                                load_na_tbl(u + 1, h)
                    else:
                        var = 0 if n == 0 else (2 if n == NOWN - 1 else 1)
                        slope = alibi_slopes(16)[u * 4 + m]

                        def bfn(e, out, scs, var=var, slope=slope):
                            return e.scalar_tensor_tensor(out, tbl[:, var * 384:(var + 1) * 384],
                                                          -slope / SCALE, scs, ALU.mult, ALU.add)
                        act_scale = SCALE
                        tkey = "tbl"
                    pvb = nxt("pv", 2)
                    if kind == 0:
                        C_, D_ = finish_simple(pvb, 128, n, u, 0)
                    else:
                        C_, D_ = finish_simple(pvb, 128, n, u * 4 + m, m * 128, sink_col=u * 4 + m)
                    if nt <= 3:
                        parts = [(0, nt)]
                    else:
                        parts = [(0, 4), (4, nt)]
                    for pi, (a_, b_) in enumerate(parts):
                        lastp = (pi == len(parts) - 1)
                        if kind == 0:
                            def bfp(e, out, scs, toff=toff, a_=a_, b_=b_):
                                return e.scalar_tensor_tensor(
                                    out, scs, SCALE, tbl_bf[:, toff + a_ * 128:toff + b_ * 128],
                                    ALU.mult, ALU.add)
                        else:
                            bfp = bfn
                        groups.append(make_group(qfn, 128, tiles[a_:b_], bfp, act_scale, tkey,
                                                 meta_k(0) if lastp else None, [(pvb, 128)],
                                                 pi == 0, lastp, C_ if lastp else None,
                                                 D_ if lastp else None, after_A if lastp else None))
            for m in range(cfg.nq):
                qfn = lambda m=m: qT[:, m, NOWN * 128:NOWN * 128 + NMETA]
                pvb = nxt("pv", 2)
                if kind == 0:
                    C_, D_ = finish_simple(pvb, NMETA, NOWN, u, 0)
                else:
                    C_, D_ = finish_simple(pvb, NMETA, NOWN, u * 4 + m, m * 128, sink_col=u * 4 + m)
                groups.append(make_group(qfn, NMETA, [], None, None, None, meta_k(0),
                                         [(pvb, NMETA)], True, True, C_, D_))
        else:
            slope = alibi_slopes(8)[u]
            C0 = 896
            chunks = [(n0 * 128, 256, [n0, n0 + 1]) for n0 in (0, 2, 4, 6)] + \
                     [(NOWN * 128, NMETA, [NOWN])]
            for (qoff, nq, tis) in chunks:
                ob = nxt("x", 2)
                nqb = min(nq, 128)
                pvl = [(i, nqb) for i in range(len(tis))]
                for m in range(2):
                    qfn = lambda m=m, qoff=qoff, nq=nq: qT[:, m, qoff:qoff + nq]
                    for g4 in range(8):
                        slots = [2 * g4 + 1 - i for i in range(2)]
                        tiles = [((lambda sl=sl, m=m: kT[:, m, sl * 128:(sl + 1) * 128]), sl)
                                 for sl in slots]
                        bfn = None
                        if nq == 256:
                            n0 = tis[0]
                            region = 0 if slots[0] < NOWN else 1
                            ks0 = slots[0] if slots[0] < NOWN else slots[0] - NOWN
                            x0 = region * 1920 + 128 * (n0 - ks0) + C0
                            t0_ = tbl[:, x0:x0 + 256]
                            win = bass.AP(t0_.tensor, t0_.offset, [list(t0_.ap[0]), [128, 2], [1, 256]])

                            def bfn(e, out, scs, win=win, slope=slope):
                                return e.scalar_tensor_tensor(
                                    out.rearrange("p (a b) -> p a b", a=2), win, -slope / SCALE,
                                    scs.rearrange("p (a b) -> p a b", a=2), ALU.mult, ALU.add)
                        groups.append(make_group(qfn, nq, tiles, bfn, SCALE, "tbl", None, pvl,
                                                 g4 == 0, False))

                    def C_(m=m, ob=ob, pvl=pvl, tis=tis):
                        for qi, (pvb, nqb_) in enumerate(pvl):
                            S.op("dve", lambda e, pvb=pvb, nqb_=nqb_: e.reciprocal(
                                small[0:nqb_, 8 + pvb:9 + pvb], pv[0:nqb_, pvb, 256:257]),
                                reads=[("pv", pvb)], writes=[("rinv", pvb)])
                            if m == 0:
                                S.op("dve", lambda e, pvb=pvb, nqb_=nqb_, qi=qi: e.tensor_scalar(
                                    obuf[0:nqb_, ob, qi, :], pv[0:nqb_, pvb, 0:256],
                                    small[0:nqb_, 8 + pvb:9 + pvb], None, ALU.mult),
                                    reads=[("pv", pvb), ("rinv", pvb)], writes=[("obuf", ob, qi)])
                            else:
                                S.op("dve", lambda e, pvb=pvb, nqb_=nqb_: e.tensor_scalar(
                                    junk[0:nqb_, :], pv[0:nqb_, pvb, 0:256],
                                    small[0:nqb_, 8 + pvb:9 + pvb], small[0:nqb_, 0:1],
                                    ALU.mult, ALU.mult),
                                    reads=[("pv", pvb), ("rinv", pvb), "small"], writes=["junk"])
                                S.op("dve", lambda e, nqb_=nqb_, qi=qi: e.tensor_tensor(
                                    obuf[0:nqb_, ob, qi, :], obuf[0:nqb_, ob, qi, :], junk[0:nqb_, :],
                                    ALU.add),
                                    reads=["junk", ("obuf", ob, qi)], writes=[("obuf", ob, qi)])
                        if m == 0:
                            return
                        for qi, ti in enumerate(tis):
                            _, ooff, nr = own_tok(ti)
                            o_ap = lambda nr=nr, qi=qi: obuf[0:nr, ob, qi, :]
                            S.op("dve", lambda e, nr=nr, o_ap=o_ap: e.tensor_tensor(
                                junk[0:nr, :], o_ap(), o_ap(), ALU.mult),
                                reads=[("obuf", ob, qi)], writes=["junk"])
                            S.op("dve", lambda e, nr=nr: e.tensor_reduce(
                                small[0:nr, 4:5], junk[0:nr, :], AX.X, ALU.add),
                                reads=["junk"], writes=["ms"])
                            S.op("dve", lambda e, nr=nr: e.tensor_scalar(
                                small[0:nr, 4:5], small[0:nr, 4:5], 1.0 / 256, RMS_EPS,
                                ALU.mult, ALU.add), reads=["ms"], writes=["ms"])
                            S.op("act", lambda e, nr=nr: e.activation(small[0:nr, 4:5],
                                                                      small[0:nr, 4:5], AF.Ln),
                                 reads=["ms"], writes=["ms"])
                            S.op("act", lambda e, nr=nr: e.activation(small[0:nr, 4:5],
                                                                      small[0:nr, 4:5], AF.Exp,
                                                                      scale=-0.5),
                                 reads=["ms"], writes=["ms"])
                            S.op("dve", lambda e, nr=nr, o_ap=o_ap: e.scalar_tensor_tensor(
                                junk[0:nr, :], o_ap(), small[0:nr, 4:5], gl[0:nr, :],
                                ALU.mult, ALU.mult),
                                reads=[("obuf", ob, qi), "ms", "gl"], writes=["junk"])
                            S.op("dve", lambda e, nr=nr, ti=ti, qi=qi: e.tensor_tensor(
                                ybuf[0:nr, qi, :], junk[0:nr, :], sz[0:nr, ti, :], ALU.mult),
                                reads=["junk", ("sz", parv)], writes=[("y", qi)])

                    D_ = None
                    if m == 1:
                        def D_(tis=tis):
                            for qi, ti in enumerate(tis):
                                _, ooff, nr = own_tok(ti)
                                for j in range(2):
                                    y_to_yT(qi, j * 128, nr, 2 * u + j, ooff)
                    groups.append(make_group(qfn, nq, [], None, None, None, meta_k(m), pvl,
                                             False, True, C_, D_))
        run_pipeline(groups)

    def exhaust(gen):
        if gen is not None:
            for _ in gen:
                pass

    if kind == 0:
        load_na_tbl(0, 0)
        load_na_tbl(0, 1)
    def chain(*gens):
        for g_ in gens:
            for _ in g_:
                yield

    if NPAR == 2:
        n_fill_ = n_fill
        exhaust(chain(proj_qk(0, 0), proj_vz(0, 0)))
        for u in range(cfg.nunits):
            p1 = (u + 1) % 2
            filler = chain(proj_qk(u + 1, p1), proj_vz(u + 1, p1)) if u + 1 < cfg.nunits else None
            attn_unit(u, u % 2, u % 2, filler)
            exhaust(filler)
    else:
        exhaust(proj_vz(0, 0))
        for u in range(cfg.nunits):
            exhaust(proj_qk(u, 0))
            filler = proj_vz(u + 1, (u + 1) % 2) if u + 1 < cfg.nunits else None
            attn_unit(u, 0, u % 2, filler)
            exhaust(filler)

    wo_sb = hT_f[:, 0:NCH * D].rearrange("p (c n) -> p c n", c=NCH)
    for nb in range(4):
        S.dma("pool", lambda e, nb=nb: e.dma_start(
            out=wo_sb[:, :, nb * 512:(nb + 1) * 512], in_=wo[nb].rearrange("p (c j) -> p c j", c=NCH)),
            writes=[("wo", nb)] + ((hT_all + [(k_, 1) for k_ in ("qT", "kT", "V", "Vones", "sz")]) if nb == 0 else []))
    lng_sb = V_f[:].bitcast(F32)[:, 0:D]
    lnb_sb = sz_f[:].bitcast(F32)[:, 0:D]
    S.dma("sp", lambda e: e.dma_start(out=lng_sb, in_=lng[L:L + 1, :].partition_broadcast(128)),
          writes=[("V", 0), ("Vones", 0)])
    S.dma("sp", lambda e: e.dma_start(out=lnb_sb, in_=lnb[L:L + 1, :].partition_broadcast(128)),
          writes=[("sz", 0)])
    xbs = [tt[:].rearrange("p a b -> p (a b)"),
           wp[:, 0:2, :].rearrange("p a b -> p (a b)").bitcast(F32)]
    xkeys = [[("tt", 0), ("tt", 1), ("tt", 2), ("tt", 3)], [("wp", 0), ("wp", 0), ("wp", 1), ("wp", 1)]]
    rbs = [qT_f[:].bitcast(F32)[:, 0:D], kT_f[:].bitcast(F32)[:, 0:D]]
    rkeys = [("qT", 0), ("kT", 0)]
    accs = [[sc[:, nb, :] for nb in range(4)], [pp[:, 0, :], pp[:, 1, :], pv[:, 0, :], pv[:, 1, :]]]
    akeys = [[("sc", nb) for nb in range(4)], [("pp", 0), ("pp", 1), ("pv", 0), ("pv", 1)]]
    junkb = PT[:].rearrange("p a b -> p (a b)")
    jkeys = [("PT", k_) for k_ in range(4)]
    if not last:
        xchunks, _ = exch_plan(KINDS[L + 1])
    for ti in range(NOWN + 1):
        off, ooff, nr = own_tok(ti)
        i = ti % 2
        xb, xk, rb, rk, acc, ak = xbs[i], xkeys[i], rbs[i], rkeys[i], accs[i], akeys[i]
        xk_u = list(dict.fromkeys(xk))
        if L == 0:
            S.dma("sp", lambda e, rb=rb, off=off, nr=nr: e.dma_start(
                out=rb[0:nr, :], in_=hwin0[off:off + nr, :]), writes=[rk])
        else:
            S.dma("sp", lambda e, rb=rb, ooff=ooff, nr=nr: e.dma_start(
                out=rb[0:nr, :], in_=hcur[L][ooff:ooff + nr, :]),
                reads=[("hcur", L, ti)], writes=[rk])
        order = ([(c, nb) for nb in range(4) for c in range(NCH)] if ti < 2 else
                 [(c, nb) for c in range(NCH) for nb in range(4)])
        for (c, nb) in order:
            S.op("pe", lambda e, c=c, nb=nb, acc=acc, ooff=ooff, nr=nr: e.matmul(
                acc[nb][0:nr, :], yT[:, c, ooff:ooff + nr], wo_sb[:, c, nb * 512:(nb + 1) * 512],
                start=(c == 0), stop=(c == NCH - 1)),
                reads=[("wo", nb), "yT"], writes=[ak[nb]])
        for nb in range(4):
            S.op("dve", lambda e, nb=nb, xb=xb, rb=rb, acc=acc, nr=nr: e.scalar_tensor_tensor(
                xb[0:nr, nb * 512:(nb + 1) * 512], rb[0:nr, nb * 512:(nb + 1) * 512], ALPHA,
                acc[nb][0:nr, :], ALU.mult, ALU.add),
                reads=[rk, ak[nb]], writes=[xk[nb]])
        S.op("dve", lambda e, xb=xb, nr=nr: e.tensor_reduce(small[0:nr, 16:17], xb[0:nr, :],
                                                            AX.X, ALU.add),
             reads=xk_u, writes=["lnm"])
        S.op("dve", lambda e, nr=nr: e.tensor_scalar(small[0:nr, 16:17], small[0:nr, 16:17],
                                                     -1.0 / D, None, ALU.mult),
             reads=["lnm"], writes=["lnm"])
        S.op("act", lambda e, xb=xb, nr=nr: e.activation(
            xb[0:nr, :], xb[0:nr, :], AF.Identity, bias=small[0:nr, 16:17]),
            reads=xk_u + ["lnm"], writes=xk_u)
        S.op("dve", lambda e: e.memset(small[:, 17:18], 0.0), writes=["lnv"])
        S.op("act", lambda e, xb=xb, nr=nr: e.activation(
            junkb[0:nr, :], xb[0:nr, :], AF.Square, accum_out=small[0:nr, 17:18]),
            reads=xk_u, writes=jkeys + ["lnv"])
        S.op("dve", lambda e, nr=nr: e.tensor_scalar(small[0:nr, 17:18], small[0:nr, 17:18],
                                                     1.0 / D, LN_EPS, ALU.mult, ALU.add),
             reads=["lnv"], writes=["lnv"])
        S.op("act", lambda e, nr=nr: e.activation(small[0:nr, 17:18], small[0:nr, 17:18], AF.Ln),
             reads=["lnv"], writes=["lnv"])
        S.op("act", lambda e, nr=nr: e.activation(small[0:nr, 17:18], small[0:nr, 17:18], AF.Exp,
                                                  scale=-0.5),
             reads=["lnv"], writes=["lnv"])
        S.op("dve", lambda e, xb=xb, nr=nr: e.scalar_tensor_tensor(
            xb[0:nr, :], xb[0:nr, :], small[0:nr, 17:18], lng_sb[0:nr, :], ALU.mult, ALU.mult),
            reads=xk_u + ["lnv", ("V", 0)], writes=xk_u)
        S.op("dve", lambda e, xb=xb, nr=nr: e.tensor_tensor(
            xb[0:nr, :], xb[0:nr, :], lnb_sb[0:nr, :], ALU.add),
            reads=xk_u + [("sz", 0)], writes=xk_u)
        if last:
            if ti < NOWN:
                S.dma("sp", lambda e, xb=xb, ooff=ooff, nr=nr: e.dma_start(
                    out=hout[ooff:ooff + nr, :], in_=xb[0:nr, :]),
                    reads=xk_u, writes=[("hout", ti)])
        else:
            S.dma("sp", lambda e, xb=xb, ooff=ooff, nr=nr: e.dma_start(
                out=hcur[L + 1][ooff:ooff + nr, :], in_=xb[0:nr, :]),
                reads=xk_u, writes=[("hcur", L + 1, ti)])
            S.dma("pool", lambda e, xb=xb, ooff=ooff, nr=nr: e.dma_start(
                out=hbf[L + 1][ooff:ooff + nr, :], in_=xb[0:nr, :]),
                reads=xk_u, writes=[("hbf", L + 1, ti)])
            for ci, sl in enumerate(xchunks):
                if ti in sl:
                    pos = sl.index(ti)
                    S.dma("pool", lambda e, xb=xb, ci=ci, pos=pos: e.dma_start(
                        out=hbx[(L + 1, ci)][pos * 128:(pos + 1) * 128, :], in_=xb[:, :]),
                        reads=xk_u, writes=[("hbx", L + 1, ci, pos)])


_NC = []


def layer_params(i, inputs):
    kind, j = i % 3, i // 3
    if kind == 0:
        return inputs["w_in_a"][j]
    if kind == 1:
        return inputs["w_in_b"][j]
    return inputs["w_in_c"][j]


def kernel(**inputs):
    inputs = {k: np.asarray(v, dtype=np.float32) for k, v in inputs.items()}
    if not _NC:
        _NC.append(build_fused())
    nc = _NC[0]
    cfgs = [Cfg(KINDS[i], i) for i in range(DEPTH)]
    x = inputs["x"]
    meta = inputs["meta_tokens"]
    shared = {"wo": np.ascontiguousarray(np.stack([prep_w_out(inputs["w_out"][i]) for i in range(DEPTH)])),
              "lng": np.ascontiguousarray(inputs["ln_g"]), "lnb": np.ascontiguousarray(inputs["ln_b"])}
    for i in range(DEPTH):
        shared["wq%d" % i] = prep_w_in(cfgs[i], layer_params(i, inputs))
    shared["lamv"] = np.ascontiguousarray(np.stack(
        [inputs["lam_q1_b"][0], inputs["lam_k1_b"][0], inputs["lam_q2_b"][0], inputs["lam_k2_b"][0]], 0))
    shared["subg"] = np.ascontiguousarray(inputs["subln_g_b"][0][None])
    shared["sink"] = np.ascontiguousarray(inputs["sink_c"][0][None])
    eye = np.eye(128, dtype=np.float32)
    zero = np.zeros((128, 128), np.float32)
    zeros_blk = np.zeros((128, D), np.float32)
    per_s = {}
    for s in range(2):
        d = {"idn": np.ascontiguousarray(np.stack([eye, eye if s == 1 else zero, eye if s == 0 else zero]))}
        for i in range(DEPTH):
            if KINDS[i] == 0:
                d["tb%d" % i] = na_bias_tables(cfgs[i], inputs["rpb_a"][i // 3], s)
            elif KINDS[i] == 1:
                d["tb%d" % i] = diff_dist_tables(s)
            else:
                d["tb%d" % i] = swa_dist_tables(s)
        per_s[s] = d
    maps = []
    for b in range(BATCH):
        for s in range(2):
            blocks = win_blocks(cfgs[0], s)
            rows = [x[b, 128 * kb:128 * (kb + 1)] if kb is not None else zeros_blk for kb in blocks]
            rows.append(meta)
            m = {"hwin0": np.ascontiguousarray(np.concatenate(rows, 0))}
            m.update(shared)
            m.update(per_s[s])
            maps.append(m)
    res = run_bass_kernel_spmd(nc, maps, core_ids=list(range(2 * BATCH)))
    out = np.empty((BATCH, SEQ, D), np.float32)
    for b in range(BATCH):
        for s in range(2):
            out[b, 1024 * s:1024 * (s + 1)] = res.results[2 * b + s]["hout"]
    return out
```

```python
import contextlib
import math
import numpy as np
import concourse.bass as bass
import concourse.mybir as mybir
from concourse.bass_utils import run_bass_kernel_spmd

F32 = mybir.dt.float32
BF16 = mybir.dt.bfloat16
AF = mybir.ActivationFunctionType
ALU = mybir.AluOpType
AX = mybir.AxisListType

D = 2048
NCH = 16
SEQ = 2048
BATCH = 4
NMETA = 16
DEPTH = 4
DH = 128
SCALE = DH ** -0.5
ALPHA = (2 * DEPTH) ** 0.25
LN_EPS = 1e-5
RMS_EPS = 1e-5
NEG = -300.0
DBIG = 1.0e5
NOWN = 8
NTOK_OWN = NOWN * 128 + NMETA

ENGS = ["pe", "act", "dve", "pool", "sp"]
NDMA = 32


class Sched:
    def __init__(self, nc):
        self.nc = nc
        self.q = {e: [] for e in ENGS}
        self.count = {e: 0 for e in ENGS}
        self.seen = {e: {} for e in ENGS}
        self.res = {}
        self.dma_next = {"sp": 0, "pool": 0, "act": 0}
        self.dma_cnt = [0] * NDMA

    def _deps(self, eng, reads, writes):
        need = {}

        def add(tok):
            if tok is None:
                return
            k, v = tok
            if need.get(k, 0) < v:
                need[k] = v

        for key in reads:
            st = self.res.get(key)
            if st:
                add(st["w"])
        for key in writes:
            st = self.res.get(key)
            if st:
                add(st["w"])
                for tok in st["r"].items():
                    add(tok)
        waits = []
        seen = self.seen[eng]
        for k, v in need.items():
            if k == eng and eng == "pe":
                continue
            if seen.get(k, 0) >= v:
                continue
            seen[k] = v
            waits.append((k, v))
        return waits

    def _mark(self, tok, reads, writes):
        k, v = tok
        for key in reads:
            st = self.res.setdefault(key, {"w": None, "r": {}})
            if st["r"].get(k, 0) < v:
                st["r"][k] = v
        for key in writes:
            self.res[key] = {"w": tok, "r": {}}

    def op(self, eng, fn, reads=(), writes=()):
        waits = self._deps(eng, reads, writes)
        self.count[eng] += 1
        idx = self.count[eng]
        self.q[eng].append(("op", fn, waits, idx))
        self._mark((eng, idx), reads, writes)
        return (eng, idx)

    def dma(self, eng, fn, reads=(), writes=()):
        half = NDMA // 2
        base = 0 if eng == "sp" else half
        slot = base + self.dma_next[eng]
        self.dma_next[eng] = (self.dma_next[eng] + 1) % half
        waits = self._deps(eng, reads, writes)
        if self.dma_cnt[slot] > 0:
            k, v = ("q%d" % slot, self.dma_cnt[slot])
            if self.seen[eng].get(k, 0) < v:
                self.seen[eng][k] = v
                waits.append((k, v))
        self.dma_cnt[slot] += 1
        tok = ("q%d" % slot, self.dma_cnt[slot])
        self.q[eng].append(("dma", fn, waits, slot))
        self._mark(tok, reads, writes)
        return tok

    def cc(self, fn, reads=(), writes=()):
        waits = self._deps("pool", reads, writes)
        self.cc_cnt = getattr(self, "cc_cnt", 0) + 1
        tok = ("cc", self.cc_cnt)
        self.q["pool"].append(("cc", fn, waits, None))
        self._mark(tok, reads, writes)
        return tok

    def barrier(self):
        for eng in ENGS:
            waits = []
            seen = self.seen[eng]
            allk = [(e, self.count[e]) for e in ENGS[:4]]
            allk += [("q%d" % i, self.dma_cnt[i]) for i in range(NDMA)]
            allk += [("cc", getattr(self, "cc_cnt", 0))]
            for k, v in allk:
                if v > 0 and seen.get(k, 0) < v:
                    seen[k] = v
                    waits.append((k, v))
            self.q[eng].append(("wait", None, waits, None))

    def wait_all(self, eng, keys):
        waits = self._deps(eng, keys, ())
        self.q[eng].append(("wait", None, waits, None))

    def emit(self):
        nc = self.nc
        with contextlib.ExitStack() as es:
            sems = {}
            for e in ENGS[:4]:
                sems[e] = es.enter_context(nc.semaphore("s_" + e))
            for i in range(NDMA):
                sems["q%d" % i] = es.enter_context(nc.semaphore("s_q%d" % i))
            sems["cc"] = es.enter_context(nc.semaphore("s_cc"))
            block = es.enter_context(nc.Block())

            def run(engname):
                def body(engine):
                    for kind, fn, waits, info in self.q[engname]:
                        for k, v in waits:
                            engine.wait_ge(sems[k], v * 16 if k.startswith("q") else v)
                        if kind == "op":
                            fn(engine).then_inc(sems[engname], 1)
                        elif kind == "dma":
                            fn(engine).then_inc(sems["q%d" % info], 16)
                        elif kind == "cc":
                            fn(engine).then_inc(sems["cc"], 1)
                return body

            block.tensor(run("pe"))
            block.scalar(run("act"))
            block.vector(run("dve"))
            block.gpsimd(run("pool"))
            block.sync(run("sp"))


def alibi_slopes(n):
    return [2.0 ** (-8.0 * (i + 1) / n) for i in range(n)]


class Cfg:
    def __init__(self, kind, layer_idx):
        self.kind = kind
        self.layer_idx = layer_idx
        if kind == 0:
            self.nunits, self.nq, self.nk, self.dv, self.zw = 16, 1, 1, 128, 128
            self.nhalo = 2
            self.dlist = [list(range(-2, 4))] + [list(range(-2, 3))] * 6 + [list(range(-3, 3))]
        elif kind == 1:
            self.nunits, self.nq, self.nk, self.dv, self.zw = 8, 2, 2, 256, 256
            self.nhalo = 8
            self.lambda_init = 0.8 - 0.6 * math.exp(-0.3 * layer_idx)
        else:
            self.nunits, self.nq, self.nk, self.dv, self.zw = 4, 4, 1, 128, 512
            self.nhalo = 2
            self.dlist = [[-1, 0, 1]] * 8
        self.nwin = NOWN + self.nhalo
        self.nwt = self.nwin * 128 + NMETA
        self.meta_off = self.nwin * 128
        self.nvb = self.dv // 128
        self.nzb = self.zw // 128
        self.blk_per_unit = self.nq + self.nk + self.nvb + self.nzb
        self.nblk = self.blk_per_unit * self.nunits

    def key_slot(self, n, dl):
        k = n + dl
        if 0 <= k < NOWN:
            return k
        if self.kind == 0:
            return {-1: 8, -2: 9, 8: 8, 9: 9}[k]
        return {-1: 8, 8: 9}[k]


def win_blocks(cfg, s):
    lo = NOWN * s
    own = list(range(lo, lo + NOWN))
    if cfg.kind == 1:
        other = list(range(NOWN * (1 - s), NOWN * (1 - s) + NOWN))
        return own + other
    if cfg.kind == 0:
        halo = [lo + NOWN, lo + NOWN + 1] if s == 0 else [lo - 1, lo - 2]
    else:
        halo = [lo - 1, lo + NOWN]
    return own + [b if 0 <= b < 16 else None for b in halo]


def unit_cols(cfg, u):
    if cfg.kind == 0:
        return [u * 128, 2048 + u * 128, 4096 + u * 128, 6144 + u * 128]
    if cfg.kind == 1:
        b = u * 256
        return [4096 + b, 4096 + b + 128, 6144 + b, 6144 + b + 128,
                b, b + 128, 2048 + b, 2048 + b + 128]
    q = [u * 512 + g * 128 for g in range(4)]
    z = [3072 + u * 512 + g * 128 for g in range(4)]
    return q + [2048 + u * 128, 2560 + u * 128] + z


def prep_w_in(cfg, w):
    cols = []
    for u in range(cfg.nunits):
        cols += unit_cols(cfg, u)
    idx = (np.asarray(cols)[:, None] + np.arange(128)[None]).reshape(-1)
    wg = np.ascontiguousarray(w[:, idx])
    wg = wg.reshape(NCH, 128, len(cols), 128).transpose(2, 1, 0, 3)
    return np.ascontiguousarray(wg).reshape(len(cols), 128, NCH * 128)


def prep_w_out(w):
    wg = w.reshape(NCH, 128, 4, 512).transpose(2, 1, 0, 3)
    return np.ascontiguousarray(wg).reshape(4, 128, NCH * 512)


def na_bias_tables(cfg, rpb, s):
    rows = 32
    kr_l = np.arange(128) // 64
    kc = np.arange(128) % 64
    qr_l = np.arange(128) // 64
    qc = np.arange(128) % 64
    cs = np.clip(qc - 8, 0, 64 - 16)
    colok = (kc[:, None] >= cs[None, :]) & (kc[:, None] < cs[None, :] + 16)
    dc = kc[:, None] - qc[None, :] + 15
    dc_c = np.clip(dc, 0, 30)
    tiles = []
    for n in range(NOWN):
        qb = NOWN * s + n
        qr = 2 * qb + qr_l
        rs = np.clip(qr - 4, 0, rows - 8)
        for dl in cfg.dlist[n]:
            kb = qb + dl
            if kb < 0 or kb > 15:
                tiles.append(np.full((16, 128, 128), NEG, np.float32))
                continue
            kr = 2 * kb + kr_l
            rowok = (kr[:, None] >= rs[None, :]) & (kr[:, None] < rs[None, :] + 8)
            dr = kr[:, None] - qr[None, :] + 7
            dr_c = np.clip(dr, 0, 14)
            vals = rpb[:, dr_c, dc_c]
            ok = (rowok & colok)[None]
            tiles.append(np.where(ok, vals, np.float32(NEG)).astype(np.float32))
    return np.ascontiguousarray(np.concatenate(tiles, axis=2))


def diff_dist_tables(s):
    C = 896
    x = np.arange(1920)[None, :]
    p = np.arange(128)[:, None]
    own = np.abs(x - C - p)
    sg = -1 if s == 0 else 1
    oth = 1024 + sg * (x - C - p)
    return np.concatenate([own, oth], axis=1).astype(np.float32)


def swa_dist_tables(s):
    p = np.arange(128)[:, None]
    q = np.arange(128)[None, :]

    def tile(dl, valid=True):
        u = q - (128 * dl + p)
        d = np.abs(u).astype(np.float32)
        d = np.where(d <= 128, d, np.float32(DBIG))
        if not valid:
            d = np.full_like(d, DBIG)
        return d

    first = [tile(-1, s == 1), tile(0), tile(1)]
    mid = [tile(-1), tile(0), tile(1)]
    last = [tile(-1), tile(0), tile(1, s == 0)]
    return np.concatenate(first + mid + last, axis=1).astype(np.float32)


KINDS = [i % 3 for i in range(DEPTH)]
PAIRS = [[0, 1], [2, 3], [4, 5], [6, 7]]


def exch_plan(kind):
    if kind == 1:
        chunks = [[0, 1, 2, 3], [4, 5, 6, 7]]
        halo = {}
        for j in range(NOWN):
            ch, k = j // 4, j % 4
            halo[NOWN + j] = [(ch, k * 128, "A"), (ch, 512 + k * 128, "B")]
        return chunks, halo
    if kind == 0:
        chunks = [[6, 7, 0, 1]]
        halo = {8: [(0, 128, "A"), (0, 512 + 256, "B")], 9: [(0, 0, "A"), (0, 512 + 384, "B")]}
        return chunks, halo
    chunks = [[7, 0]]
    halo = {8: [(0, 0, "A")], 9: [(0, 256 + 128, "B")]}
    return chunks, halo


def build_fused():
    nc = bass.Bass("TRN2", target_bir_lowering=False)
    cfgs = [Cfg(KINDS[i], i) for i in range(DEPTH)]
    dt = lambda name, shape, k="ExternalInput", d=F32: nc.dram_tensor(name, shape, d, kind=k).ap()
    hwin0 = dt("hwin0", [cfgs[0].nwt, D])
    wqs = [dt("wq%d" % i, [cfgs[i].nblk, 128, NCH * 128]) for i in range(DEPTH)]
    wos = dt("wo", [DEPTH, 4, 128, NCH * 512])
    lng = dt("lng", [DEPTH, D])
    lnb = dt("lnb", [DEPTH, D])
    idn = dt("idn", [3, 128, 128])
    hout = dt("hout", [NOWN * 128, D], "ExternalOutput")
    pre = dt("pre", [NTOK_OWN, D], "Internal")
    tbs = {}
    for i in range(DEPTH):
        if KINDS[i] == 0:
            nt_ = sum(len(x) for x in cfgs[i].dlist)
            tbs[i] = dt("tb%d" % i, [16, 128, nt_ * 128])
        elif KINDS[i] == 1:
            tbs[i] = dt("tb%d" % i, [128, 2 * 1920])
        else:
            tbs[i] = dt("tb%d" % i, [128, 9 * 128])
    lamv = dt("lamv", [4, 128])
    subg = dt("subg", [1, 256])
    sink = dt("sink", [1, 16])
    hcur = {L: dt("hcur%d" % L, [NTOK_OWN, D], "Internal") for L in range(1, DEPTH)}
    hbf = {L: dt("hbf%d" % L, [NTOK_OWN, D], "Internal", BF16) for L in range(1, DEPTH)}
    hbx, hgt = {}, {}
    for L in range(1, DEPTH):
        chunks, _ = exch_plan(KINDS[L])
        for ci, sl in enumerate(chunks):
            hbx[(L, ci)] = nc.dram_tensor("hbx%d_%d" % (L, ci), [len(sl) * 128, D], BF16)
            hgt[(L, ci)] = nc.dram_tensor("hgt%d_%d" % (L, ci), [2 * len(sl) * 128, D], BF16)

    NW = 6
    mx = lambda f: max(f(c) for c in cfgs)
    with contextlib.ExitStack() as es:
        T = lambda name, shape, d=F32: es.enter_context(nc.sbuf_tensor(name, shape, d))
        P = lambda name, shape, d=F32: es.enter_context(nc.psum_tensor(name, shape, d))
        hT_f = T("hT", [128, NCH * mx(lambda c: c.nwt)], BF16)
        yT = T("yT", [128, NCH, NTOK_OWN], BF16)
        wp = T("wp", [128, NW, NCH * 128], BF16)
        qT_f = T("qT", [128, mx(lambda c: c.nq) * NTOK_OWN], BF16)
        kT_f = T("kT", [128, mx(lambda c: c.nk * c.nwt)], BF16)
        V_f = T("V", [128, mx(lambda c: (c.nwin + 1) * (c.dv + 2))], BF16)
        sz_f = T("sz", [128, mx(lambda c: (NOWN + 1) * c.zw)], BF16)
        PT = T("PT", [128, 4, 512], BF16)
        PTm = None
        tt = T("tt", [128, 4, 512], F32)
        vz1 = T("vz1", [128, 6700], BF16)
        stg = vz1[:, 0:3 * D].rearrange("p (a b) -> p a b", a=3)
        ident3 = T("ident", [128, 3, 128], BF16)
        ybuf = T("ybuf", [128, 2, 256], BF16)
        small = T("small", [128, 64], F32)
        NTB = 42 * 128
        tbl_f = T("tbl", [128, 2 * 1920], F32)
        obuf = T("obuf", [128, 2, 2, 256], F32)
        lamt = T("lamt", [128, 4, 128], F32)
        gl = T("gl", [128, 256], F32)
        junk = T("junk", [128, 256], F32)
        esink = T("esink", [128, 16], F32)
        rbuf = qT_f[:].bitcast(F32)[:, 0:1024].rearrange("p (a b) -> p a b", a=2)
        rbuf2 = kT_f[:].bitcast(F32)[:, 0:1024].rearrange("p (a b) -> p a b", a=2)

        pp = P("pp", [128, 2, 512], F32)
        sc = P("sc", [128, 4, 512], F32)
        pv = P("pv", [128, 2, 512], F32)
        ppb = pp[:].bitcast(BF16)
        ident = ident3[:, 0, :]
        maskid = {"A": ident3[:, 1, :], "B": ident3[:, 2, :]}

        S = Sched(nc)
        cnt = {"pp": 0, "sc": 0, "pv": 0, "wp": 0, "stg": 0, "y": 0, "x": 0, "r": 0}

        def nxt(name, n):
            v = cnt[name] % n
            cnt[name] += 1
            return v

        S.dma("pool", lambda e: e.dma_start(out=ident3[:], in_=idn.rearrange("a p j -> p a j")),
              writes=["ident"])

        for L in range(DEPTH):
            emit_layer(nc, S, L, cfgs[L], locals())
            if L < DEPTH - 1:
                S.barrier()
        S.wait_all("sp", [("hout", ti) for ti in range(NOWN)])
        S.emit()
    return nc


def emit_layer(nc, S, L, cfg, E):
    kind = cfg.kind
    NWT = cfg.nwt
    nxt = E["nxt"]
    hT_f, yT, wp, qT_f, kT_f, V_f, sz_f = E["hT_f"], E["yT"], E["wp"], E["qT_f"], E["kT_f"], E["V_f"], E["sz_f"]
    PT, PTm, tt, stg, ident, maskid, ybuf, small = (E["PT"], E["PTm"], E["tt"], E["stg"], E["ident"],
                                                   E["maskid"], E["ybuf"], E["small"])
    tbl_f, obuf, lamt, gl, junk, esink, rbuf, rbuf2 = (E["tbl_f"], E["obuf"], E["lamt"], E["gl"], E["junk"],
                                                       E["esink"], E["rbuf"], E["rbuf2"])
    vz1 = E["vz1"]
    pp, sc, pv, ppb = E["pp"], E["sc"], E["pv"], E["ppb"]
    wq, wo, tb = E["wqs"][L], E["wos"][L], E["tbs"][L]
    lng, lnb, pre, hout = E["lng"], E["lnb"], E["pre"], E["hout"]
    lamv, subg, sink = E["lamv"], E["subg"], E["sink"]
    hwin0, hcur, hbf, hbx, hgt = E["hwin0"], E["hcur"], E["hbf"], E["hbx"], E["hgt"]
    NW = E["NW"]
    NTB = E["NTB"]
    last = (L == DEPTH - 1)

    hT = hT_f[:, 0:NCH * NWT].rearrange("p (c t) -> p c t", c=NCH)
    qT = qT_f[:, 0:cfg.nq * NTOK_OWN].rearrange("p (m t) -> p m t", m=cfg.nq)
    kT = kT_f[:, 0:cfg.nk * NWT].rearrange("p (m t) -> p m t", m=cfg.nk)
    V = V_f[:, 0:(cfg.nwin + 1) * (cfg.dv + 2)].rearrange("p (s d) -> p s d", s=cfg.nwin + 1)
    sz = sz_f[:, 0:(NOWN + 1) * cfg.zw].rearrange("p (t z) -> p t z", t=NOWN + 1)
    UBQ = [(qT, kT)]
    UBV = [(V, sz)]
    if kind != 1:
        o0 = NCH * NWT
        n_q, n_k = cfg.nq * NTOK_OWN, cfg.nk * NWT
        n_v, n_z = (cfg.nwin + 1) * (cfg.dv + 2), (NOWN + 1) * cfg.zw
        assert o0 + n_q + n_k + n_v + n_z <= NCH * 2064
        qT1 = hT_f[:, o0:o0 + n_q].rearrange("p (m t) -> p m t", m=cfg.nq)
        kT1 = hT_f[:, o0 + n_q:o0 + n_q + n_k].rearrange("p (m t) -> p m t", m=cfg.nk)
        V1 = hT_f[:, o0 + n_q + n_k:o0 + n_q + n_k + n_v].rearrange("p (s d) -> p s d", s=cfg.nwin + 1)
        sz1 = hT_f[:, o0 + n_q + n_k + n_v:o0 + n_q + n_k + n_v + n_z].rearrange(
            "p (t z) -> p t z", t=NOWN + 1)
        UBQ.append((qT1, kT1))
        UBV.append((V1, sz1))
    else:
        n_v, n_z = (cfg.nwin + 1) * (cfg.dv + 2), (NOWN + 1) * cfg.zw
        V1 = vz1[:, 0:n_v].rearrange("p (s d) -> p s d", s=cfg.nwin + 1)
        sz1 = vz1[:, n_v:n_v + n_z].rearrange("p (t z) -> p t z", t=NOWN + 1)
        UBV.append((V1, sz1))
    NPAR = len(UBQ)
    tbl = tbl_f
    if kind == 0:
        ntile_tot = sum(len(x) for x in cfg.dlist)
        tbl_bf = tbl_f[:].bitcast(BF16)
        nhalf = sum(len(x) for x in cfg.dlist[:4]) * 128

    if L > 0:
        xch_in, _ = exch_plan(kind)
        for ci, sl in enumerate(xch_in):
            S.cc(lambda e, ci=ci: e.collective_compute(
                "AllGather", ALU.bypass, replica_groups=PAIRS,
                ins=[hbx[(L, ci)].ap().opt()], outs=[hgt[(L, ci)].ap().opt()]),
                reads=[("hbx", L, ci, pos) for pos in range(len(sl))],
                writes=[("hgt", L, ci)])

    if kind == 1:
        S.dma("sp", lambda e: e.dma_start(out=tbl[:, 0:2 * 1920], in_=tb), writes=["tbl"])
        for i in range(4):
            S.dma("sp", lambda e, i=i: e.dma_start(
                out=lamt[:, i, :], in_=lamv[i:i + 1, :].partition_broadcast(128)), writes=["lamt"])
        S.dma("sp", lambda e: e.dma_start(out=gl[:], in_=subg.partition_broadcast(128)),
              writes=["gl"])
        S.op("act", lambda e: e.mul(gl[:], gl[:], 1.0 - cfg.lambda_init), reads=["gl"], writes=["gl"])
        for j in range(2):
            S.op("dve", lambda e, j=j: e.tensor_tensor(junk[:, 0:128], lamt[:, 2 * j, :],
                                                       lamt[:, 2 * j + 1, :], ALU.mult),
                 reads=["lamt"], writes=["junk"])
            S.op("dve", lambda e, j=j: e.tensor_reduce(small[:, 1 + j:2 + j], junk[:, 0:128],
                                                       AX.X, ALU.add),
                 reads=["junk"], writes=["small"])
        S.op("act", lambda e: e.activation(small[:, 1:3], small[:, 1:3], AF.Exp),
             reads=["small"], writes=["small"])
        S.op("dve", lambda e: e.scalar_tensor_tensor(small[:, 0:1], small[:, 2:3],
                                                     -cfg.lambda_init, small[:, 1:2],
                                                     ALU.add, ALU.subtract),
             reads=["small"], writes=["small"])
    elif kind == 2:
        S.dma("sp", lambda e: e.dma_start(out=tbl[:, 0:9 * 128], in_=tb), writes=["tbl"])
        S.dma("sp", lambda e: e.dma_start(out=esink[:], in_=sink.partition_broadcast(128)),
              writes=["esink"])
        S.op("act", lambda e: e.activation(esink[:], esink[:], AF.Exp),
             reads=["esink"], writes=["esink"])

    def tok_rows(slot):
        if slot < cfg.nwin:
            return slot * 128, 128
        return cfg.meta_off, NMETA

    def own_tok(ti):
        if ti < NOWN:
            return ti * 128, ti * 128, 128
        return cfg.meta_off, NOWN * 128, NMETA

    def transpose_rows(slot, sb, nr, rkeys):
        r0, _ = tok_rows(slot)
        for half in range(2):
            pb = nxt("pp", 2)
            for c8 in range(8):
                c = half * 8 + c8
                S.op("pe", lambda e, pb=pb, c8=c8, c=c: e.transpose(
                    ppb[:, pb, c8 * 128:c8 * 128 + nr], stg[0:nr, sb, c * 128:(c + 1) * 128],
                    ident[0:nr, 0:nr]),
                    reads=rkeys + ["ident"], writes=[("pp", pb)])
            src = lambda pb=pb: ppb[:, pb, :].rearrange("p (c t) -> p c t", c=8)[:, :, 0:nr]
            dst = lambda half=half: hT[:, half * 8:half * 8 + 8, r0:r0 + nr]
            if half == 0:
                S.op("dve", lambda e, src=src, dst=dst: e.tensor_copy(dst(), src()),
                     reads=[("pp", pb)], writes=[("hT", slot)])
            else:
                S.op("act", lambda e, src=src, dst=dst: e.copy(dst(), src()),
                     reads=[("pp", pb)], writes=[("hT", slot)])

    if L == 0:
        for slot in range(cfg.nwin + 1):
            r0, nr = tok_rows(slot)
            sb = nxt("stg", 3)
            S.dma("pool", lambda e, sb=sb, r0=r0, nr=nr: e.dma_start(
                out=stg[0:nr, sb, :], in_=hwin0[r0:r0 + nr, :]), writes=[("stg", sb)])
            transpose_rows(slot, sb, nr, [("stg", sb)])
    else:
        _, halo = exch_plan(kind)
        for ti in range(NOWN + 1):
            _, ooff, nr = own_tok(ti)
            slot = ti if ti < NOWN else cfg.nwin
            sb = nxt("stg", 3)
            S.dma("sp", lambda e, sb=sb, ooff=ooff, nr=nr: e.dma_start(
                out=stg[0:nr, sb, :], in_=hbf[L][ooff:ooff + nr, :]),
                reads=[("hbf", L, ti)], writes=[("stg", sb)])
            transpose_rows(slot, sb, nr, [("stg", sb)])
        for slot in sorted(halo):
            cands = halo[slot]
            sbs = []
            for (ci, roff, mk) in cands:
                sb = nxt("stg", 3)
                sbs.append(sb)
                S.dma("sp", lambda e, sb=sb, ci=ci, roff=roff: e.dma_start(
                    out=stg[:, sb, :], in_=hgt[(L, ci)][roff:roff + 128, :]),
                    reads=[("hgt", L, ci)], writes=[("stg", sb)])
            r0 = slot * 128
            for q4 in range(4):
                pb = nxt("pp", 2)
                for c4 in range(4):
                    c = q4 * 4 + c4
                    for k, (ci, roff, mk) in enumerate(cands):
                        S.op("pe", lambda e, pb=pb, c4=c4, c=c, k=k, mk=mk, sb=sbs[k]: e.matmul(
                            pp[:, pb, c4 * 128:(c4 + 1) * 128], stg[:, sb, c * 128:(c + 1) * 128],
                            maskid[mk], start=(k == 0), stop=(k == len(cands) - 1)),
                            reads=[("stg", sbs[k]), "ident"], writes=[("pp", pb)])
                src = lambda pb=pb: pp[:, pb, :].rearrange("p (c t) -> p c t", c=4)
                dst = lambda q4=q4, r0=r0: hT[:, q4 * 4:q4 * 4 + 4, r0:r0 + 128]
                if q4 % 2 == 0:
                    S.op("dve", lambda e, src=src, dst=dst: e.tensor_copy(dst(), src()),
                         reads=[("pp", pb)], writes=[("hT", slot)])
                else:
                    S.op("act", lambda e, src=src, dst=dst: e.copy(dst(), src()),
                         reads=[("pp", pb)], writes=[("hT", slot)])

    hT_all = [("hT", s_) for s_ in range(cfg.nwin + 1)]

    for par_ in range(len(UBV)):
        S.op("pool", lambda e, par_=par_: e.memset(UBV[par_][0][:, :, cfg.dv:cfg.dv + 2], 1.0),
             writes=[("Vones", par_), ("V", par_), ("stg", 0), ("stg", 1), ("stg", 2)])

    wb_of = {}
    w_next = [0]

    def load_w(blk):
        upto = min(blk + 3, cfg.nblk)
        while w_next[0] < upto:
            bi = w_next[0]
            wb = nxt("wp", NW)
            wb_of[bi] = wb
            S.dma("pool", lambda e, wb=wb, bi=bi: e.dma_start(out=wp[:, wb, :], in_=wq[bi]),
                  writes=[("wp", wb)])
            w_next[0] += 1
        return wb_of[blk]

    def wview(wb):
        return wp[:, wb, :].rearrange("p (c j) -> p c j", c=NCH)

    def evac(eng, dst, src, reads, writes):
        if eng == "dve":
            S.op("dve", lambda e: e.tensor_copy(dst(), src()), reads=reads, writes=writes)
        else:
            S.op("act", lambda e: e.copy(dst(), src()), reads=reads, writes=writes)

    ev_rr = [0]

    def ev_eng():
        ev_rr[0] += 1
        return "dve" if ev_rr[0] % 2 else "act"

    def proj_fm(wb, dst_fn, tok_chunks, hkeys, wkey_dst):
        w = wview(wb)
        for (off, doff, n) in tok_chunks:
            pb = nxt("pp", 2)
            for c in range(NCH):
                S.op("pe", lambda e, pb=pb, c=c, off=off, n=n, w=w: e.matmul(
                    pp[:, pb, 0:n], w[:, c, :], hT[:, c, off:off + n],
                    start=(c == 0), stop=(c == NCH - 1)),
                    reads=[("wp", wb)] + hkeys, writes=[("pp", pb)])
            evac(ev_eng(), (lambda doff=doff, n=n: dst_fn(doff, n)),
                 (lambda pb=pb, n=n: pp[:, pb, 0:n]), [("pp", pb)], [wkey_dst])
            yield

    own_chunks = [(0, 0, 512), (512, 512, 512), (cfg.meta_off, NOWN * 128, NMETA)]
    win_chunks = []
    o_ = 0
    while o_ < cfg.nwin * 128:
        n_ = min(512, cfg.nwin * 128 - o_)
        win_chunks.append((o_, o_, n_))
        o_ += n_
    win_chunks.append((cfg.meta_off, cfg.meta_off, NMETA))

    def load_na_tbl(u_, h):
        c0, c1 = (0, nhalf) if h == 0 else (nhalf, ntile_tot * 128)
        S.dma("pool", lambda e: e.dma_start(out=tbl_bf[:, c0:c1], in_=tb[u_][:, c0:c1]),
              writes=[("tbl", h)])

    n_fill = cfg.nwin + 1 + (2 + cfg.nq * len(own_chunks) + cfg.nk * len(win_chunks) if kind != 1 else 0)

    def proj_qk(u, par):
        qT, kT = UBQ[par]
        b = u * cfg.blk_per_unit + (cfg.nvb + cfg.nzb if kind == 1 else 0)
        for m in range(cfg.nq):
            wb = load_w(b); b += 1
            for _ in proj_fm(wb, (lambda doff, n, m=m: qT[:, m, doff:doff + n]), own_chunks, hT_all,
                             ("qT", par)):
                yield
        for m in range(cfg.nk):
            wb = load_w(b); b += 1
            for _ in proj_fm(wb, (lambda doff, n, m=m: kT[:, m, doff:doff + n]), win_chunks, hT_all,
                             ("kT", par)):
                yield
        yield

    def proj_vz(u, parv):
        V, sz = UBV[parv]
        b = u * cfg.blk_per_unit + (0 if kind == 1 else cfg.nq + cfg.nk)
        wvs = []
        for j in range(cfg.nvb):
            wvs.append(load_w(b)); b += 1
        for slot in range(cfg.nwin + 1):
            r0, nr = tok_rows(slot)
            pb = nxt("pp", 2)
            for j, wb in enumerate(wvs):
                w = wview(wb)
                for c in range(NCH):
                    S.op("pe", lambda e, pb=pb, c=c, r0=r0, nr=nr, w=w, j=j: e.matmul(
                        pp[0:nr, pb, j * 128:(j + 1) * 128], hT[:, c, r0:r0 + nr], w[:, c, :],
                        start=(c == 0), stop=(c == NCH - 1)),
                        reads=[("wp", wb)] + hT_all, writes=[("pp", pb)])
            evac(ev_eng(), (lambda slot=slot, nr=nr: V[0:nr, slot, 0:cfg.dv]),
                 (lambda pb=pb, nr=nr: pp[0:nr, pb, 0:cfg.dv]), [("pp", pb)], [("V", parv)])
            yield
        wzs = []
        for j in range(cfg.nzb):
            wzs.append(load_w(b)); b += 1
        for ti in range(NOWN + 1):
            off, _, nr = own_tok(ti)
            pb = nxt("pp", 2)
            for j, wb in enumerate(wzs):
                w = wview(wb)
                for c in range(NCH):
                    S.op("pe", lambda e, pb=pb, c=c, off=off, nr=nr, w=w, j=j: e.matmul(
                        pp[0:nr, pb, j * 128:(j + 1) * 128], hT[:, c, off:off + nr], w[:, c, :],
                        start=(c == 0), stop=(c == NCH - 1)),
                        reads=[("wp", wb)] + hT_all, writes=[("pp", pb)])
            S.op("act", lambda e, pb=pb, nr=nr, ti=ti: e.activation(
                sz[0:nr, ti, :], pp[0:nr, pb, 0:cfg.zw], AF.Silu),
                reads=[("pp", pb)], writes=[("sz", parv)])

        yield

    def attn_unit(u, par, parv, filler):
        qT, kT = UBQ[par]
        V, sz = UBV[parv]
        class G:
            pass

        def make_group(qap_fn, nq, tiles, bias, act_scale, tkey, meta_k, pv_spec, first, last_,
                       C=None, D=None, after_A=None):
            g = G()
            st = {}
            W = len(tiles) * nq

            def A():
                sb = nxt("sc", 4)
                st["sb"] = sb
                for i, (kfn, _) in enumerate(tiles):
                    S.op("pe", lambda e, i=i, kfn=kfn: e.matmul(
                        sc[:, sb, i * nq:(i + 1) * nq], kfn(), qap_fn(), start=True, stop=True),
                        reads=[("qT", par), ("kT", par)], writes=[("sc", sb)])
                if meta_k is not None:
                    S.op("pe", lambda e: e.matmul(sc[0:NMETA, sb, W:W + nq], meta_k(), qap_fn(),
                                                  start=True, stop=True),
                         reads=[("qT", par), ("kT", par)], writes=[("sc", sb)])
                if tiles:
                    if bias is not None:
                        S.op("dve", lambda e: bias(e, tt[:, sb, 0:W], sc[:, sb, 0:W]),
                             reads=[("sc", sb), tkey], writes=[("tt", sb)])
                        S.op("act", lambda e: e.activation(PT[:, sb, 0:W], tt[:, sb, 0:W], AF.Exp,
                                                           scale=act_scale),
                             reads=[("tt", sb)], writes=[("PT", sb)])
                    else:
                        S.op("act", lambda e: e.activation(PT[:, sb, 0:W], sc[:, sb, 0:W], AF.Exp,
                                                           scale=SCALE),
                             reads=[("sc", sb)], writes=[("PT", sb)])
                if meta_k is not None:
                    S.op("act", lambda e: e.activation(PT[0:NMETA, sb, W:W + nq], sc[0:NMETA, sb, W:W + nq],
                                                       AF.Exp, scale=SCALE),
                         reads=[("sc", sb)], writes=[("PT", sb)])
                if after_A is not None:
                    after_A()

            def B():
                sb = st["sb"]
                ops = [("PT", PT, i * nq, slot, 128) for i, (_, slot) in enumerate(tiles)]
                if meta_k is not None:
                    ops.append(("PT", PT, W, cfg.nwin, NMETA))
                for qb, (pvb, nqb) in enumerate(pv_spec):
                    for j, (nm, buf, c0, slot, nk) in enumerate(ops):
                        S.op("pe", lambda e, qb=qb, pvb=pvb, nqb=nqb, buf=buf, c0=c0, slot=slot, nk=nk, j=j:
                             e.matmul(pv[0:nqb, pvb, 0:cfg.dv + 1],
                                      buf[0:nk, sb, c0 + qb * 128:c0 + qb * 128 + nqb],
                                      V[0:nk, slot, 0:cfg.dv + 1],
                                      start=(first and j == 0), stop=(last_ and j == len(ops) - 1)),
                             reads=[(nm, sb), ("V", parv), ("Vones", parv)], writes=[("pv", pvb)])

            g.A, g.B, g.C, g.D = A, B, C, D
            return g

        def run_pipeline(groups, LA=3):
            n = len(groups)
            if n == 0:
                return
            rate = 0.0
            if filler is not None:
                rate = (n_fill + 2) / max(1, n - 3)
            fill_acc = [0.0]
            for i in range(min(LA, n)):
                groups[i].A()
            for i in range(n):
                early = groups[i].C is not None
                if i + LA < n and not early:
                    groups[i + LA].A()
                groups[i].B()
                if groups[i].C is not None:
                    groups[i].C()
                if i + LA < n and early:
                    groups[i + LA].A()
                if i >= 1 and groups[i - 1].D is not None:
                    groups[i - 1].D()
                fill_acc[0] += rate
                while fill_acc[0] >= 1.0:
                    next(filler, None)
                    fill_acc[0] -= 1.0
            if groups[n - 1].D is not None:
                groups[n - 1].D()

        def y_to_yT(yb, col0, nqb, chunk, ooff):
            pb = nxt("pp", 2)
            S.op("pe", lambda e: e.transpose(ppb[:, pb, 0:nqb], ybuf[0:nqb, yb, col0:col0 + 128],
                                             ident[0:nqb, 0:nqb]),
                 reads=[("y", yb), "ident"], writes=[("pp", pb)])
            S.op("act", lambda e: e.copy(yT[:, chunk, ooff:ooff + nqb], ppb[:, pb, 0:nqb]),
                 reads=[("pp", pb)], writes=["yT"])

        def finish_simple(pvb, nqb, ti, unit_chunk, zoff, sink_col=None):
            st = {}
            _, ooff, _ = own_tok(ti)

            def C():
                yb = nxt("y", 2)
                st["yb"] = yb
                if sink_col is None:
                    S.op("dve", lambda e: e.reciprocal(small[0:nqb, 8 + pvb:9 + pvb],
                                                       pv[0:nqb, pvb, cfg.dv:cfg.dv + 1]),
                         reads=[("pv", pvb)], writes=[("rinv", pvb)])
                else:
                    S.op("dve", lambda e: e.tensor_tensor(small[0:nqb, 8 + pvb:9 + pvb],
                                                          pv[0:nqb, pvb, cfg.dv:cfg.dv + 1],
                                                          esink[0:nqb, sink_col:sink_col + 1], ALU.add),
                         reads=[("pv", pvb), "esink"], writes=[("rinv", pvb)])
                    S.op("dve", lambda e: e.reciprocal(small[0:nqb, 8 + pvb:9 + pvb],
                                                       small[0:nqb, 8 + pvb:9 + pvb]),
                         reads=[("rinv", pvb)], writes=[("rinv", pvb)])
                S.op("dve", lambda e: e.scalar_tensor_tensor(
                    ybuf[0:nqb, yb, 0:128], pv[0:nqb, pvb, 0:128], small[0:nqb, 8 + pvb:9 + pvb],
                    sz[0:nqb, ti, zoff:zoff + 128], ALU.mult, ALU.mult),
                    reads=[("pv", pvb), ("rinv", pvb), ("sz", parv)], writes=[("y", yb)])

            def Dd():
                y_to_yT(st["yb"], 0, nqb, unit_chunk, ooff)

            return C, Dd


        meta_k = lambda m: (lambda m=m: kT[:, m, cfg.meta_off:cfg.meta_off + NMETA])
        groups = []
        if kind in (0, 2):
            for n in range(NOWN):
                dls = cfg.dlist[n]
                slots = [cfg.key_slot(n, dl) for dl in dls]
                nt = len(dls)
                for m in range(cfg.nq):
                    qfn = lambda n=n, m=m: qT[:, m, n * 128:(n + 1) * 128]
                    tiles = [((lambda sl=sl: kT[:, 0, sl * 128:(sl + 1) * 128]), sl) for sl in slots]
                    after_A = None
                    if kind == 0:
                        toff = sum(len(x) for x in cfg.dlist[:n]) * 128

                        def bfn(e, out, scs, toff=toff, nt=nt):
                            return e.scalar_tensor_tensor(out, scs, SCALE,
                                                          tbl_bf[:, toff:toff + nt * 128],
                                                          ALU.mult, ALU.add)
                        act_scale = 1.0
                        tkey = ("tbl", n // 4)
                        if n % 4 == 3 and u + 1 < cfg.nunits:
                            def after_A(h=n // 4, u=u):
                                load_na_tbl(u + 1, h)
                    else:
                        var = 0 if n == 0 else (2 if n == NOWN - 1 else 1)
                        slope = alibi_slopes(16)[u * 4 + m]

                        def bfn(e, out, scs, var=var, slope=slope):
                            return e.scalar_tensor_tensor(out, tbl[:, var * 384:(var + 1) * 384],
                                                          -slope / SCALE, scs, ALU.mult, ALU.add)
                        act_scale = SCALE
                        tkey = "tbl"
                    pvb = nxt("pv", 2)
                    if kind == 0:
                        C_, D_ = finish_simple(pvb, 128, n, u, 0)
                    else:
                        C_, D_ = finish_simple(pvb, 128, n, u * 4 + m, m * 128, sink_col=u * 4 + m)
                    if nt <= 3:
                        parts = [(0, nt)]
                    else:
                        parts = [(0, 4), (4, nt)]
                    for pi, (a_, b_) in enumerate(parts):
                        lastp = (pi == len(parts) - 1)
                        if kind == 0:
                            def bfp(e, out, scs, toff=toff, a_=a_, b_=b_):
                                return e.scalar_tensor_tensor(
                                    out, scs, SCALE, tbl_bf[:, toff + a_ * 128:toff + b_ * 128],
                                    ALU.mult, ALU.add)
                        else:
                            bfp = bfn
                        groups.append(make_group(qfn, 128, tiles[a_:b_], bfp, act_scale, tkey,
                                                 meta_k(0) if lastp else None, [(pvb, 128)],
                                                 pi == 0, lastp, C_ if lastp else None,
                                                 D_ if lastp else None, after_A if lastp else None))
            for m in range(cfg.nq):
                qfn = lambda m=m: qT[:, m, NOWN * 128:NOWN * 128 + NMETA]
                pvb = nxt("pv", 2)
                if kind == 0:
                    C_, D_ = finish_simple(pvb, NMETA, NOWN, u, 0)
                else:
                    C_, D_ = finish_simple(pvb, NMETA, NOWN, u * 4 + m, m * 128, sink_col=u * 4 + m)
                groups.append(make_group(qfn, NMETA, [], None, None, None, meta_k(0),
                                         [(pvb, NMETA)], True, True, C_, D_))
        else:
            slope = alibi_slopes(8)[u]
            C0 = 896
            chunks = [(n0 * 128, 256, [n0, n0 + 1]) for n0 in (0, 2, 4, 6)] + \
                     [(NOWN * 128, NMETA, [NOWN])]
            for (qoff, nq, tis) in chunks:
                ob = nxt("x", 2)
                nqb = min(nq, 128)
                pvl = [(i, nqb) for i in range(len(tis))]
                for m in range(2):
                    qfn = lambda m=m, qoff=qoff, nq=nq: qT[:, m, qoff:qoff + nq]
                    for g4 in range(8):
                        slots = [2 * g4 + 1 - i for i in range(2)]
                        tiles = [((lambda sl=sl, m=m: kT[:, m, sl * 128:(sl + 1) * 128]), sl)
                                 for sl in slots]
                        bfn = None
                        if nq == 256:
                            n0 = tis[0]
                            region = 0 if slots[0] < NOWN else 1
                            ks0 = slots[0] if slots[0] < NOWN else slots[0] - NOWN
                            x0 = region * 1920 + 128 * (n0 - ks0) + C0
                            t0_ = tbl[:, x0:x0 + 256]
                            win = bass.AP(t0_.tensor, t0_.offset, [list(t0_.ap[0]), [128, 2], [1, 256]])

                            def bfn(e, out, scs, win=win, slope=slope):
                                return e.scalar_tensor_tensor(
                                    out.rearrange("p (a b) -> p a b", a=2), win, -slope / SCALE,
                                    scs.rearrange("p (a b) -> p a b", a=2), ALU.mult, ALU.add)
                        groups.append(make_group(qfn, nq, tiles, bfn, SCALE, "tbl", None, pvl,
                                                 g4 == 0, False))

                    def C_(m=m, ob=ob, pvl=pvl, tis=tis):
                        for qi, (pvb, nqb_) in enumerate(pvl):
                            S.op("dve", lambda e, pvb=pvb, nqb_=nqb_: e.reciprocal(
                                small[0:nqb_, 8 + pvb:9 + pvb], pv[0:nqb_, pvb, 256:257]),
                                reads=[("pv", pvb)], writes=[("rinv", pvb)])
                            if m == 0:
                                S.op("dve", lambda e, pvb=pvb, nqb_=nqb_, qi=qi: e.tensor_scalar(
                                    obuf[0:nqb_, ob, qi, :], pv[0:nqb_, pvb, 0:256],
                                    small[0:nqb_, 8 + pvb:9 + pvb], None, ALU.mult),
                                    reads=[("pv", pvb), ("rinv", pvb)], writes=[("obuf", ob, qi)])
                            else:
                                S.op("dve", lambda e, pvb=pvb, nqb_=nqb_: e.tensor_scalar(
                                    junk[0:nqb_, :], pv[0:nqb_, pvb, 0:256],
                                    small[0:nqb_, 8 + pvb:9 + pvb], small[0:nqb_, 0:1],
                                    ALU.mult, ALU.mult),
                                    reads=[("pv", pvb), ("rinv", pvb), "small"], writes=["junk"])
                                S.op("dve", lambda e, nqb_=nqb_, qi=qi: e.tensor_tensor(
                                    obuf[0:nqb_, ob, qi, :], obuf[0:nqb_, ob, qi, :], junk[0:nqb_, :],
                                    ALU.add),
                                    reads=["junk", ("obuf", ob, qi)], writes=[("obuf", ob, qi)])
                        if m == 0:
                            return
                        for qi, ti in enumerate(tis):
                            _, ooff, nr = own_tok(ti)
                            o_ap = lambda nr=nr, qi=qi: obuf[0:nr, ob, qi, :]
                            S.op("dve", lambda e, nr=nr, o_ap=o_ap: e.tensor_tensor(
                                junk[0:nr, :], o_ap(), o_ap(), ALU.mult),
                                reads=[("obuf", ob, qi)], writes=["junk"])
                            S.op("dve", lambda e, nr=nr: e.tensor_reduce(
                                small[0:nr, 4:5], junk[0:nr, :], AX.X, ALU.add),
                                reads=["junk"], writes=["ms"])
                            S.op("dve", lambda e, nr=nr: e.tensor_scalar(
                                small[0:nr, 4:5], small[0:nr, 4:5], 1.0 / 256, RMS_EPS,
                                ALU.mult, ALU.add), reads=["ms"], writes=["ms"])
                            S.op("act", lambda e, nr=nr: e.activation(small[0:nr, 4:5],
                                                                      small[0:nr, 4:5], AF.Ln),
                                 reads=["ms"], writes=["ms"])
                            S.op("act", lambda e, nr=nr: e.activation(small[0:nr, 4:5],
                                                                      small[0:nr, 4:5], AF.Exp,
                                                                      scale=-0.5),
                                 reads=["ms"], writes=["ms"])
                            S.op("dve", lambda e, nr=nr, o_ap=o_ap: e.scalar_tensor_tensor(
                                junk[0:nr, :], o_ap(), small[0:nr, 4:5], gl[0:nr, :],
                                ALU.mult, ALU.mult),
                                reads=[("obuf", ob, qi), "ms", "gl"], writes=["junk"])
                            S.op("dve", lambda e, nr=nr, ti=ti, qi=qi: e.tensor_tensor(
                                ybuf[0:nr, qi, :], junk[0:nr, :], sz[0:nr, ti, :], ALU.mult),
                                reads=["junk", ("sz", parv)], writes=[("y", qi)])

                    D_ = None
                    if m == 1:
                        def D_(tis=tis):
                            for qi, ti in enumerate(tis):
                                _, ooff, nr = own_tok(ti)
                                for j in range(2):
                                    y_to_yT(qi, j * 128, nr, 2 * u + j, ooff)
                    groups.append(make_group(qfn, nq, [], None, None, None, meta_k(m), pvl,
                                             False, True, C_, D_))
        run_pipeline(groups)

    def exhaust(gen):
        if gen is not None:
            for _ in gen:
                pass

    if kind == 0:
        load_na_tbl(0, 0)
        load_na_tbl(0, 1)
    def chain(*gens):
        for g_ in gens:
            for _ in g_:
                yield

    if NPAR == 2:
        n_fill_ = n_fill
        exhaust(chain(proj_qk(0, 0), proj_vz(0, 0)))
        for u in range(cfg.nunits):
            p1 = (u + 1) % 2
            filler = chain(proj_qk(u + 1, p1), proj_vz(u + 1, p1)) if u + 1 < cfg.nunits else None
            attn_unit(u, u % 2, u % 2, filler)
            exhaust(filler)
    else:
        exhaust(proj_vz(0, 0))
        for u in range(cfg.nunits):
            exhaust(proj_qk(u, 0))
            filler = proj_vz(u + 1, (u + 1) % 2) if u + 1 < cfg.nunits else None
            attn_unit(u, 0, u % 2, filler)
            exhaust(filler)

    wo_sb = hT_f[:, 0:NCH * D].rearrange("p (c n) -> p c n", c=NCH)
    for nb in range(4):
        S.dma("pool", lambda e, nb=nb: e.dma_start(
            out=wo_sb[:, :, nb * 512:(nb + 1) * 512], in_=wo[nb].rearrange("p (c j) -> p c j", c=NCH)),
            writes=[("wo", nb)] + ((hT_all + [(k_, 1) for k_ in ("qT", "kT", "V", "Vones", "sz")]) if nb == 0 else []))
    lng_sb = V_f[:].bitcast(F32)[:, 0:D]
    lnb_sb = sz_f[:].bitcast(F32)[:, 0:D]
    S.dma("sp", lambda e: e.dma_start(out=lng_sb, in_=lng[L:L + 1, :].partition_broadcast(128)),
          writes=[("V", 0), ("Vones", 0)])
    S.dma("sp", lambda e: e.dma_start(out=lnb_sb, in_=lnb[L:L + 1, :].partition_broadcast(128)),
          writes=[("sz", 0)])
    xbs = [tt[:].rearrange("p a b -> p (a b)"),
           wp[:, 0:2, :].rearrange("p a b -> p (a b)").bitcast(F32)]
    xkeys = [[("tt", 0), ("tt", 1), ("tt", 2), ("tt", 3)], [("wp", 0), ("wp", 0), ("wp", 1), ("wp", 1)]]
    rbs = [qT_f[:].bitcast(F32)[:, 0:D], kT_f[:].bitcast(F32)[:, 0:D]]
    rkeys = [("qT", 0), ("kT", 0)]
    accs = [[sc[:, nb, :] for nb in range(4)], [pp[:, 0, :], pp[:, 1, :], pv[:, 0, :], pv[:, 1, :]]]
    akeys = [[("sc", nb) for nb in range(4)], [("pp", 0), ("pp", 1), ("pv", 0), ("pv", 1)]]
    junkb = PT[:].rearrange("p a b -> p (a b)")
    jkeys = [("PT", k_) for k_ in range(4)]
    if not last:
        xchunks, _ = exch_plan(KINDS[L + 1])
    for ti in range(NOWN + 1):
        off, ooff, nr = own_tok(ti)
        i = ti % 2
        xb, xk, rb, rk, acc, ak = xbs[i], xkeys[i], rbs[i], rkeys[i], accs[i], akeys[i]
        xk_u = list(dict.fromkeys(xk))
        if L == 0:
            S.dma("sp", lambda e, rb=rb, off=off, nr=nr: e.dma_start(
                out=rb[0:nr, :], in_=hwin0[off:off + nr, :]), writes=[rk])
        else:
            S.dma("sp", lambda e, rb=rb, ooff=ooff, nr=nr: e.dma_start(
                out=rb[0:nr, :], in_=hcur[L][ooff:ooff + nr, :]),
                reads=[("hcur", L, ti)], writes=[rk])
        order = ([(c, nb) for nb in range(4) for c in range(NCH)] if ti < 2 else
                 [(c, nb) for c in range(NCH) for nb in range(4)])
        for (c, nb) in order:
            S.op("pe", lambda e, c=c, nb=nb, acc=acc, ooff=ooff, nr=nr: e.matmul(
                acc[nb][0:nr, :], yT[:, c, ooff:ooff + nr], wo_sb[:, c, nb * 512:(nb + 1) * 512],
                start=(c == 0), stop=(c == NCH - 1)),
                reads=[("wo", nb), "yT"], writes=[ak[nb]])
        for nb in range(4):
            S.op("dve", lambda e, nb=nb, xb=xb, rb=rb, acc=acc, nr=nr: e.scalar_tensor_tensor(
                xb[0:nr, nb * 512:(nb + 1) * 512], rb[0:nr, nb * 512:(nb + 1) * 512], ALPHA,
                acc[nb][0:nr, :], ALU.mult, ALU.add),
                reads=[rk, ak[nb]], writes=[xk[nb]])
        S.op("dve", lambda e, xb=xb, nr=nr: e.tensor_reduce(small[0:nr, 16:17], xb[0:nr, :],
                                                            AX.X, ALU.add),
             reads=xk_u, writes=["lnm"])
        S.op("dve", lambda e, nr=nr: e.tensor_scalar(small[0:nr, 16:17], small[0:nr, 16:17],
                                                     -1.0 / D, None, ALU.mult),
             reads=["lnm"], writes=["lnm"])
        S.op("act", lambda e, xb=xb, nr=nr: e.activation(
            xb[0:nr, :], xb[0:nr, :], AF.Identity, bias=small[0:nr, 16:17]),
            reads=xk_u + ["lnm"], writes=xk_u)
        S.op("dve", lambda e: e.memset(small[:, 17:18], 0.0), writes=["lnv"])
        S.op("act", lambda e, xb=xb, nr=nr: e.activation(
            junkb[0:nr, :], xb[0:nr, :], AF.Square, accum_out=small[0:nr, 17:18]),
            reads=xk_u, writes=jkeys + ["lnv"])
        S.op("dve", lambda e, nr=nr: e.tensor_scalar(small[0:nr, 17:18], small[0:nr, 17:18],
                                                     1.0 / D, LN_EPS, ALU.mult, ALU.add),
             reads=["lnv"], writes=["lnv"])
        S.op("act", lambda e, nr=nr: e.activation(small[0:nr, 17:18], small[0:nr, 17:18], AF.Ln),
             reads=["lnv"], writes=["lnv"])
        S.op("act", lambda e, nr=nr: e.activation(small[0:nr, 17:18], small[0:nr, 17:18], AF.Exp,
                                                  scale=-0.5),
             reads=["lnv"], writes=["lnv"])
        S.op("dve", lambda e, xb=xb, nr=nr: e.scalar_tensor_tensor(
            xb[0:nr, :], xb[0:nr, :], small[0:nr, 17:18], lng_sb[0:nr, :], ALU.mult, ALU.mult),
            reads=xk_u + ["lnv", ("V", 0)], writes=xk_u)
        S.op("dve", lambda e, xb=xb, nr=nr: e.tensor_tensor(
            xb[0:nr, :], xb[0:nr, :], lnb_sb[0:nr, :], ALU.add),
            reads=xk_u + [("sz", 0)], writes=xk_u)
        if last:
            if ti < NOWN:
                S.dma("sp", lambda e, xb=xb, ooff=ooff, nr=nr: e.dma_start(
                    out=hout[ooff:ooff + nr, :], in_=xb[0:nr, :]),
                    reads=xk_u, writes=[("hout", ti)])
        else:
            S.dma("sp", lambda e, xb=xb, ooff=ooff, nr=nr: e.dma_start(
                out=hcur[L + 1][ooff:ooff + nr, :], in_=xb[0:nr, :]),
                reads=xk_u, writes=[("hcur", L + 1, ti)])
            S.dma("pool", lambda e, xb=xb, ooff=ooff, nr=nr: e.dma_start(
                out=hbf[L + 1][ooff:ooff + nr, :], in_=xb[0:nr, :]),
                reads=xk_u, writes=[("hbf", L + 1, ti)])
            for ci, sl in enumerate(xchunks):
                if ti in sl:
                    pos = sl.index(ti)
                    S.dma("pool", lambda e, xb=xb, ci=ci, pos=pos: e.dma_start(
                        out=hbx[(L + 1, ci)][pos * 128:(pos + 1) * 128, :], in_=xb[:, :]),
                        reads=xk_u, writes=[("hbx", L + 1, ci, pos)])


_NC = []


def layer_params(i, inputs):
    kind, j = i % 3, i // 3
    if kind == 0:
        return inputs["w_in_a"][j]
    if kind == 1:
        return inputs["w_in_b"][j]
    return inputs["w_in_c"][j]


def kernel(**inputs):
    inputs = {k: np.asarray(v, dtype=np.float32) for k, v in inputs.items()}
    if not _NC:
        _NC.append(build_fused())
    nc = _NC[0]
    cfgs = [Cfg(KINDS[i], i) for i in range(DEPTH)]
    x = inputs["x"]
    meta = inputs["meta_tokens"]
    shared = {"wo": np.ascontiguousarray(np.stack([prep_w_out(inputs["w_out"][i]) for i in range(DEPTH)])),
              "lng": np.ascontiguousarray(inputs["ln_g"]), "lnb": np.ascontiguousarray(inputs["ln_b"])}
    for i in range(DEPTH):
        shared["wq%d" % i] = prep_w_in(cfgs[i], layer_params(i, inputs))
    shared["lamv"] = np.ascontiguousarray(np.stack(
        [inputs["lam_q1_b"][0], inputs["lam_k1_b"][0], inputs["lam_q2_b"][0], inputs["lam_k2_b"][0]], 0))
    shared["subg"] = np.ascontiguousarray(inputs["subln_g_b"][0][None])
    shared["sink"] = np.ascontiguousarray(inputs["sink_c"][0][None])
    eye = np.eye(128, dtype=np.float32)
    zero = np.zeros((128, 128), np.float32)
    zeros_blk = np.zeros((128, D), np.float32)
    per_s = {}
    for s in range(2):
        d = {"idn": np.ascontiguousarray(np.stack([eye, eye if s == 1 else zero, eye if s == 0 else zero]))}
        for i in range(DEPTH):
            if KINDS[i] == 0:
                d["tb%d" % i] = na_bias_tables(cfgs[i], inputs["rpb_a"][i // 3], s)
            elif KINDS[i] == 1:
                d["tb%d" % i] = diff_dist_tables(s)
            else:
                d["tb%d" % i] = swa_dist_tables(s)
        per_s[s] = d
    maps = []
    for b in range(BATCH):
        for s in range(2):
            blocks = win_blocks(cfgs[0], s)
            rows = [x[b, 128 * kb:128 * (kb + 1)] if kb is not None else zeros_blk for kb in blocks]
            rows.append(meta)
            m = {"hwin0": np.ascontiguousarray(np.concatenate(rows, 0))}
            m.update(shared)
            m.update(per_s[s])
            maps.append(m)
    res = run_bass_kernel_spmd(nc, maps, core_ids=list(range(2 * BATCH)))
    out = np.empty((BATCH, SEQ, D), np.float32)
    for b in range(BATCH):
        for s in range(2):
            out[b, 1024 * s:1024 * (s + 1)] = res.results[2 * b + s]["hout"]
    return out
```
